# Optimizing a Trainium2 kernel written in Bass

```python
import math
import jax, jax.numpy as jnp
from jax import lax
import numpy as np

D_MODEL = 2048
BATCH = 4
SEQ = 8192
DEPTH = 1

GRID_W = 64
CTX_LEN = 256
MIX_WIDTH = D_MODEL
DA_HEADS = 8
DA_QK_DIM = 64
DA_V_DIM = 2 * DA_QK_DIM
DA_WIDTH = DA_HEADS * DA_V_DIM
NA_HEADS = 8
NA_HEAD_DIM = 128
NA_WIDTH = NA_HEADS * NA_HEAD_DIM
NA_WIN_ROWS = 8
NA_WIN_COLS = 16
Q_BLOCK = 128
ROPE_BASE = 10000.0
NORM_EPS = 1e-6
SUBLN_EPS = 1e-5
IN_COLS = 4 * DA_WIDTH + 4 * NA_WIDTH
NA_OFF = 4 * DA_WIDTH

kernel_name = "hybrid_diffattn_natten_dit_layer"


def rmsnorm(x, g, eps=NORM_EPS):
    xf = x.astype(jnp.float32)
    y = xf * lax.rsqrt(jnp.mean(xf * xf, axis=-1, keepdims=True) + eps)
    return (y * g.astype(jnp.float32)).astype(x.dtype)


def rope_1d(x, pos):
    half = x.shape[-1] // 2
    inv_freq = ROPE_BASE ** (-jnp.arange(half, dtype=jnp.float32) / half)
    ang = pos.astype(jnp.float32)[:, None] * inv_freq[None, :]
    cos = jnp.cos(ang).astype(x.dtype)
    sin = jnp.sin(ang).astype(x.dtype)
    x1, x2 = x[..., :half], x[..., half:]
    return jnp.concatenate([x1 * cos - x2 * sin, x2 * cos + x1 * sin], axis=-1)


def axial_rope(x, row_pos, col_pos):
    a = x.shape[-1] // 2
    return jnp.concatenate([rope_1d(x[..., :a], row_pos), rope_1d(x[..., a:], col_pos)], axis=-1)


def diff_attention(q, k, v, k_ctx, v_ctx, row_pos, col_pos, lam, subln_g, lambda_init):
    B, S, _ = q.shape
    n_ctx = k_ctx.shape[1]
    split_qk = lambda t, n: t.reshape(B, n, DA_HEADS, 2, DA_QK_DIM).transpose(0, 2, 3, 1, 4)
    split_v = lambda t, n: t.reshape(B, n, DA_HEADS, DA_V_DIM).transpose(0, 2, 1, 3)
    qh = axial_rope(split_qk(q, S), row_pos, col_pos) * (DA_QK_DIM ** -0.5)
    kh = axial_rope(split_qk(k, S), row_pos, col_pos)
    k_all = jnp.concatenate([kh, split_qk(k_ctx, n_ctx)], axis=3)
    v_all = jnp.concatenate([split_v(v, S), split_v(v_ctx, n_ctx)], axis=2)
    nb = S // Q_BLOCK
    q_blocks = jnp.moveaxis(qh.reshape(B, DA_HEADS, 2, nb, Q_BLOCK, DA_QK_DIM), 3, 0)

    def block(qb):
        s = jnp.einsum('bhiqd,bhikd->bhiqk', qb, k_all).astype(jnp.float32)
        p = jax.nn.softmax(s, axis=-1)
        attn = p[:, :, 0] - lam * p[:, :, 1]
        return jnp.einsum('bhqk,bhkd->bhqd', attn.astype(v_all.dtype), v_all)

    out = lax.map(block, q_blocks)
    out = out.transpose(1, 0, 3, 2, 4).reshape(B, S, DA_HEADS, DA_V_DIM)
    out = rmsnorm(out, subln_g, SUBLN_EPS) * (1.0 - lambda_init)
    return out.reshape(B, S, DA_WIDTH)


def neighbourhood_attention(q, k, v, k_ctx, v_ctx, rpb_l, rows):
    B, S, _ = q.shape
    n_ctx = k_ctx.shape[1]
    kr = min(NA_WIN_ROWS, rows)
    kc = min(NA_WIN_COLS, GRID_W)
    grid = lambda t: t.reshape(B, rows, GRID_W, NA_HEADS, NA_HEAD_DIM).transpose(0, 3, 1, 2, 4)
    qg = grid(q) * (NA_HEAD_DIM ** -0.5)
    kg = grid(k)
    vg = grid(v)
    kcx = k_ctx.reshape(B, n_ctx, NA_HEADS, NA_HEAD_DIM).transpose(0, 2, 1, 3)
    vcx = v_ctx.reshape(B, n_ctx, NA_HEADS, NA_HEAD_DIM).transpose(0, 2, 1, 3)
    j = np.arange(GRID_W)
    c0 = np.clip(j - kc // 2, 0, GRID_W - kc)
    col_idx_np = c0[:, None] + np.arange(kc)[None, :]
    col_off_np = col_idx_np - j[:, None] + (NA_WIN_COLS - 1)
    col_idx = jnp.asarray(col_idx_np, dtype=jnp.int32)
    bias_cols = rpb_l[:, :, jnp.asarray(col_off_np, dtype=jnp.int32)]
    n_win = kr * kc

    def row_block(args):
        r, q_row = args
        r0 = jnp.clip(r - kr // 2, 0, rows - kr)
        k_band = lax.dynamic_slice_in_dim(kg, r0, kr, axis=2)
        v_band = lax.dynamic_slice_in_dim(vg, r0, kr, axis=2)
        k_win = k_band[:, :, :, col_idx]
        v_win = v_band[:, :, :, col_idx]
        row_off = r0 + jnp.arange(kr, dtype=jnp.int32) - r + (NA_WIN_ROWS - 1)
        bias = jnp.transpose(bias_cols[:, row_off], (0, 2, 1, 3))
        s_win = jnp.einsum('bhqd,bhrqcd->bhqrc', q_row, k_win).astype(jnp.float32) + bias.astype(jnp.float32)
        s_ctx = jnp.einsum('bhqd,bhcd->bhqc', q_row, kcx).astype(jnp.float32)
        s = jnp.concatenate([s_win.reshape(B, NA_HEADS, GRID_W, n_win), s_ctx], axis=-1)
        p = jax.nn.softmax(s, axis=-1).astype(v.dtype)
        p_win = p[..., :n_win].reshape(B, NA_HEADS, GRID_W, kr, kc)
        return (jnp.einsum('bhqrc,bhrqcd->bhqd', p_win, v_win)
                + jnp.einsum('bhqc,bhcd->bhqd', p[..., n_win:], vcx))

    out = lax.map(row_block, (jnp.arange(rows, dtype=jnp.int32), jnp.moveaxis(qg, 2, 0)))
    return out.transpose(1, 0, 3, 2, 4).reshape(B, S, NA_WIDTH)


def setup_inputs(seed: int = 0) -> dict:
    key = jax.random.key(seed)
    ks = jax.random.split(key, 16)
    f32 = jnp.float32
    nrm = lambda k, shape, s: jax.random.normal(k, shape, f32) * s
    return {
        "x": nrm(ks[0], (BATCH, SEQ, D_MODEL), 1.0),
        "c": nrm(ks[1], (BATCH, D_MODEL), 1.0),
        "ctx": nrm(ks[2], (BATCH, CTX_LEN, D_MODEL), 1.0),
        "c_ctx": nrm(ks[3], (D_MODEL,), 1.0),
        "norm_g": 1.0 + nrm(ks[4], (DEPTH, D_MODEL), 0.02),
        "w_mod": nrm(ks[5], (DEPTH, D_MODEL, 3 * D_MODEL), 0.5 * D_MODEL ** -0.5),
        "b_mod": nrm(ks[6], (DEPTH, 3 * D_MODEL), 0.02),
        "w_in": nrm(ks[7], (DEPTH, D_MODEL, IN_COLS), D_MODEL ** -0.5),
        "w_out": nrm(ks[8], (DEPTH, MIX_WIDTH, D_MODEL), MIX_WIDTH ** -0.5),
        "lam_q1": nrm(ks[9], (DEPTH, DA_QK_DIM), 0.1),
        "lam_k1": nrm(ks[10], (DEPTH, DA_QK_DIM), 0.1),
        "lam_q2": nrm(ks[11], (DEPTH, DA_QK_DIM), 0.1),
        "lam_k2": nrm(ks[12], (DEPTH, DA_QK_DIM), 0.1),
        "subln_g": 1.0 + nrm(ks[13], (DEPTH, DA_V_DIM), 0.02),
        "rpb": nrm(ks[14], (DEPTH, NA_HEADS, 2 * NA_WIN_ROWS - 1, 2 * NA_WIN_COLS - 1), 0.1),
        "final_g": 1.0 + nrm(ks[15], (D_MODEL,), 0.02),
    }


def reference(x, c, ctx, c_ctx, norm_g, w_mod, b_mod, w_in, w_out, lam_q1, lam_k1, lam_q2, lam_k2, subln_g, rpb, final_g):
    B, S, _ = x.shape
    rows = S // GRID_W
    t = jnp.arange(S, dtype=jnp.int32)
    row_pos = t // GRID_W
    col_pos = t % GRID_W
    h = x
    for layer in range(DEPTH):
        lambda_init = 0.8 - 0.6 * math.exp(-0.3 * layer)
        wm, bm, wi = w_mod[layer], b_mod[layer], w_in[layer]
        shift, scale, gate = jnp.split(jax.nn.silu(c) @ wm + bm, 3, axis=-1)
        shift_c, scale_c = jnp.split(jax.nn.silu(c_ctx) @ wm[:, :2 * D_MODEL] + bm[:2 * D_MODEL], 2, axis=-1)
        hx = rmsnorm(h, norm_g[layer]) * (1.0 + scale[:, None]) + shift[:, None]
        hc = rmsnorm(ctx, norm_g[layer]) * (1.0 + scale_c) + shift_c
        proj = hx @ wi
        qa, ka, va, ga, qb, kb, vb, gb = jnp.split(proj, 8, axis=-1)
        ka_c, va_c = jnp.split(hc @ wi[:, DA_WIDTH:3 * DA_WIDTH], 2, axis=-1)
        kb_c, vb_c = jnp.split(hc @ wi[:, NA_OFF + NA_WIDTH:NA_OFF + 3 * NA_WIDTH], 2, axis=-1)
        lq1, lk1 = lam_q1[layer].astype(jnp.float32), lam_k1[layer].astype(jnp.float32)
        lq2, lk2 = lam_q2[layer].astype(jnp.float32), lam_k2[layer].astype(jnp.float32)
        lam = jnp.exp(jnp.sum(lq1 * lk1)) - jnp.exp(jnp.sum(lq2 * lk2)) + lambda_init
        oa = diff_attention(qa, ka, va, ka_c, va_c, row_pos, col_pos, lam, subln_g[layer], lambda_init)
        oa = oa * jax.nn.silu(ga)
        ob = neighbourhood_attention(qb, kb, vb, kb_c, vb_c, rpb[layer], rows) * jax.nn.silu(gb)
        mixed = jnp.concatenate([oa, ob], axis=-1) @ w_out[layer]
        h = h + gate[:, None] * mixed
    return rmsnorm(h, final_g)
```

```python
import numpy as np
from concourse.bass_utils import run_bass_kernel_spmd
from contextlib import ExitStack
import concourse.bass as bass
import concourse.mybir as mybir

F32 = mybir.dt.float32
BF16 = mybir.dt.bfloat16
U8 = mybir.dt.uint8
AF = mybir.ActivationFunctionType
ALU = mybir.AluOpType
AX = mybir.AxisListType


class _Op:
    __slots__ = ("eng", "fn", "deps", "dma", "signal", "ev", "oid")

    def __init__(self, eng, fn, deps, dma, oid):
        self.eng = eng
        self.fn = fn
        self.deps = deps
        self.dma = dma
        self.signal = False
        self.ev = None
        self.oid = oid


class Prog:
    COMPUTE = ("pe", "act", "dve", "pool")
    EPOCH = 30000
    NDMA = 8

    def __init__(self, nc, same_engine_sync=True):
        self.nc = nc
        self.ops = []
        self.res_w = {}
        self.res_r = {}
        self.same = same_engine_sync
        self.dma_cnt = {}
        self.dma_last = {}
        self.barrier_deps = set()
        self.last_op = {}
        self.all_dma = []
        self.bank_last = {}

    limit = None

    def add(self, eng, fn, reads=(), writes=(), dma=False, banks=()):
        oid = len(self.ops)
        if self.limit is not None and oid >= self.limit:
            return None
        deps = set(self.barrier_deps)
        for b in banks:
            last = self.bank_last.get(b)
            if last is not None and self.ops[last].eng != eng:
                deps.add(last)
            self.bank_last[b] = oid
        for r in reads:
            w = self.res_w.get(r)
            if w is not None:
                deps.add(w)
        for wr in writes:
            w = self.res_w.get(wr)
            if w is not None:
                deps.add(w)
            for rd in self.res_r.get(wr, ()):
                deps.add(rd)
        if dma:
            i = self.dma_cnt.get(eng, 0)
            self.dma_cnt[eng] = i + 1
            key = (eng, i % self.NDMA)
            prev = self.dma_last.get(key)
            if prev is not None:
                deps.add(prev)
            self.dma_last[key] = oid
        op = _Op(eng, fn, deps, dma, oid)
        if self.limit is not None:
            print("OP", oid, eng, "dma" if dma else "", list(writes)[:2])
        if dma:
            op.ev = ("dma", key, 16 * (i // self.NDMA + 1))
            op.signal = True
            self.all_dma.append(oid)
        self.ops.append(op)
        for r in reads:
            self.res_r.setdefault(r, []).append(oid)
        for wr in writes:
            self.res_w[wr] = oid
            self.res_r[wr] = []
        self.last_op[eng] = oid
        return oid

    def barrier(self):
        deps = set(self.dma_last.values())
        for e, o in self.last_op.items():
            if not self.ops[o].dma:
                deps.add(o)
        self.barrier_deps = deps

    def _needs_sync(self, op, dep):
        if dep.dma:
            return True
        if dep.eng != op.eng:
            return True
        if op.dma:
            return True
        if op.eng == "pe":
            return False
        return self.same

    def finalize(self, es):
        nc = self.nc
        ops = self.ops
        for op in ops:
            for d in op.deps:
                dep = ops[d]
                if not dep.dma and self._needs_sync(op, dep):
                    dep.signal = True
        cnt = {e: 0 for e in self.COMPUTE}
        for op in ops:
            if op.dma or not op.signal:
                continue
            c = cnt[op.eng]
            op.ev = ("eng", op.eng, c // self.EPOCH, c % self.EPOCH + 1)
            cnt[op.eng] = c + 1
        self.sems = {}
        for e in self.COMPUTE:
            n_ep = (cnt[e] + self.EPOCH - 1) // self.EPOCH
            for k in range(max(n_ep, 1)):
                self.sems[("eng", e, k)] = es.enter_context(nc.semaphore(f"s_{e}_{k}"))
        for q, n in self.dma_cnt.items():
            for k in range(min(n, self.NDMA)):
                self.sems[("dma", (q, k))] = es.enter_context(nc.semaphore(f"d_{q}_{k}"))
        self.n_signal = cnt

    def emit_engine(self, eng, e):
        ops = self.ops
        seen_eng = {}
        seen_dma = {}
        for op in ops:
            if op.eng != eng:
                continue
            need_eng = {}
            need_dma = {}
            for d in op.deps:
                dep = ops[d]
                if not self._needs_sync(op, dep):
                    continue
                ev = dep.ev
                if ev[0] == "dma":
                    if need_dma.get(ev[1], 0) < ev[2]:
                        need_dma[ev[1]] = ev[2]
                else:
                    cur = need_eng.get(ev[1])
                    if cur is None or cur < (ev[2], ev[3]):
                        need_eng[ev[1]] = (ev[2], ev[3])
            for k, v in need_dma.items():
                if seen_dma.get(k, 0) < v:
                    e.wait_ge(self.sems[("dma", k)], v)
                    seen_dma[k] = v
            for k, v in need_eng.items():
                cur = seen_eng.get(k)
                if cur is None or cur < v:
                    e.wait_ge(self.sems[("eng", k, v[0])], v[1])
                    seen_eng[k] = v
            ins = op.fn(e)
            if op.signal:
                if op.dma:
                    ins.then_inc(self.sems[("dma", op.ev[1])], 16)
                else:
                    ins.then_inc(self.sems[("eng", op.eng, op.ev[2])], 1)

    def emit(self, block):
        P = self

        @block.sync
        def _(e):
            P.emit_engine("sp", e)

        @block.tensor
        def _(e):
            P.emit_engine("pe", e)

        @block.scalar
        def _(e):
            P.emit_engine("act", e)

        @block.vector
        def _(e):
            P.emit_engine("dve", e)

        @block.gpsimd
        def _(e):
            P.emit_engine("pool", e)


class Arena:
    def __init__(self, tensor, nbytes):
        self.t = tensor
        self.n = nbytes
        self.off = 0

    def reset(self, off=0):
        self.off = off

    def alloc(self, shape, dt):
        esz = mybir.dt.size(dt)
        free = int(np.prod(shape[1:]))
        nb = free * esz
        a = (self.off + 63) // 64 * 64
        assert a + nb <= self.n, f"arena overflow: {a + nb} > {self.n}"
        self.off = a + nb
        v = self.t[:, a:a + nb]
        if dt != U8:
            v = v.bitcast(dt)
        if len(shape) > 2:
            names = " ".join(f"d{i}" for i in range(1, len(shape)))
            kw = {f"d{i}": int(shape[i]) for i in range(2, len(shape))}
            v = v.rearrange(f"p ({names}) -> p {names}", **kw)
        if shape[0] != 128:
            v = v[0:shape[0]]
        return v
NT_OWN = 4096
NT_ALL = 8192
NCTX = 256
NKA = NT_ALL + NCTX
NKB = NT_OWN + 256 + NCTX
LAMBDA_INIT = 0.8 - 0.6 * 1.0
QSCALE_A = 64 ** -0.5
QSCALE_B = 128 ** -0.5
NEG = -30000.0


def build_nc(debug=False, phases="ABCDE"):
    nc = bass.Bass("TRN2", target_bir_lowering=False)

    def din(name, shape, dt=F32):
        return nc.dram_tensor(name, list(shape), dt, kind="ExternalInput").ap()

    def dscr(name, shape, dt=BF16):
        return nc.dram_tensor(name, list(shape), dt, kind=("ExternalOutput" if debug else "Internal")).ap()

    x_d = din("x_loc", [NT_ALL, 2048])
    ctx_d = din("ctx", [NCTX, 2048])
    cvec_d = din("cvec", [128, 16, 2])
    wmod_d = din("w_mod", [2048, 6144])
    bmod_d = din("b_mod2", [128, 48])
    ng_d = din("norm_g2", [128, 16])
    win_d = din("w_in", [2048, 8192])
    wout_d = din("w_out", [2048, 2048])
    lamv_d = din("lamv", [128, 4, 64])
    subg_d = din("sublng", [128, 1024])
    fgb_d = din("fgb", [128, 2048])
    ropeR_d = din("ropeR", [128, 2, 128])
    ropeC_d = din("ropeC", [128, 2, 64])
    perm_d = din("permm", [128, 128])
    ident_d = din("ident", [128, 128])
    nab_d = din("nab", [8, 128, 3, 640])
    nam_d = din("nam", [128, 3, 640])
    out_d = nc.dram_tensor("out_loc", [NT_OWN, 2048], F32, kind="ExternalOutput").ap()

    qaT_s = dscr("qaT_s", [8, 128, NT_OWN])
    kaT_s = dscr("kaT_s", [8, 128, NKA])
    vA_s = dscr("vA_s", [NKA, 1024])
    gA_s = dscr("gA_s", [NT_OWN, 1024])
    qbT_s = dscr("qbT_s", [8, 128, NT_OWN])
    kbT_s = dscr("kbT_s", [8, 128, NKB])
    vB_s = dscr("vB_s", [NKB, 1024])
    gB_s = dscr("gB_s", [NT_OWN, 1024])
    mixed_s = dscr("mixed_s", [NT_OWN, 2048])
    gate_s = dscr("gate_s", [2048], F32)
    winb_s = nc.dram_tensor("winb_s", [16, 128, 16, 512], BF16, kind="Internal").ap()
    woutb_s = nc.dram_tensor("woutb_s", [4, 128, 16, 512], BF16, kind="Internal").ap()

    es = ExitStack()
    ARENA = 200 * 1024
    art = es.enter_context(nc.sbuf_tensor("arena", [128, ARENA], U8))
    A = Arena(art, ARENA)
    ps_t = es.enter_context(nc.psum_tensor("psall", [128, 8, 512], F32))
    ps_bf_t = ps_t.bitcast(BF16)
    ps = ps_t
    psb = ps_bf_t

    Pg = Prog(nc)
    add = Pg.add
    import os as _os0
    if _os0.environ.get("K_MAXOPS"):
        Pg.limit = int(_os0.environ["K_MAXOPS"])

    ident = A.alloc([128, 128], BF16)
    perm = A.alloc([128, 128], BF16)
    modv = A.alloc([128, 48, 2], F32)
    a_x = A.alloc([128, 16], F32)
    a_c = A.alloc([128, 16], F32)
    nlam = A.alloc([128, 1], F32)
    base_persist = A.off

    add("pool", lambda e: e.dma_start(out=ident, in_=ident_d), writes=["ident"], dma=True)
    add("pool", lambda e: e.dma_start(out=perm, in_=perm_d), writes=["perm"], dma=True)
    def conv_w(cst):
        n = 0
        for (srcw, dstw, nblk, key) in ((win_d, winb_s, 16, "winb"), (wout_d, woutb_s, 4, "woutb")):
            for blk in range(nblk):
                sl = n % 2
                n += 1
                src = srcw[:, blk * 512:(blk + 1) * 512].rearrange("(kc p) n -> p kc n", p=128)
                add("pool", lambda e, sl=sl, src=src: e.dma_start(out=cst[sl], in_=src), writes=[("cst", sl)], dma=True)
                add("sp", lambda e, sl=sl, blk=blk, dstw=dstw: e.dma_start(out=dstw[blk], in_=cst[sl]), reads=[("cst", sl)], writes=[(key, blk)], dma=True)

    if "A" in phases:
        cv = A.alloc([128, 16, 2], F32)
        sc = A.alloc([128, 16, 2], BF16)
        bm = A.alloc([128, 48], F32)
        ng = A.alloc([128, 16], F32)
        lamv = A.alloc([128, 4, 64], F32)
        lprod = A.alloc([128, 2, 64], F32)
        lsum = A.alloc([128, 2], F32)
        wm = [A.alloc([128, 6144], BF16) for _ in range(3)]
        add("sp", lambda e: e.dma_start(out=cv, in_=cvec_d), writes=["cv"], dma=True)
        add("sp", lambda e: e.dma_start(out=bm, in_=bmod_d), writes=["bm"], dma=True)
        add("sp", lambda e: e.dma_start(out=ng, in_=ng_d), writes=["ng"], dma=True)
        add("sp", lambda e: e.dma_start(out=lamv, in_=lamv_d), writes=["lamv"], dma=True)
        add("act", lambda e: e.activation(out=sc, in_=cv, func=AF.Silu), reads=["cv"], writes=["sc"])
        for kc in range(16):
            s = kc % 3
            add("pool", lambda e, s=s, kc=kc: e.dma_start(out=wm[s], in_=wmod_d[kc * 128:(kc + 1) * 128, :]),
                writes=[("wm", s)], dma=True)

            def mm(e, s=s, kc=kc):
                ins = None
                for j in range(48):
                    ins = e.matmul(ps[:, 7, 2 * j:2 * j + 2], lhsT=wm[s][:, j * 128:(j + 1) * 128], rhs=sc[:, kc, :],
                                   start=(kc == 0 and j == 0), stop=(kc == 15), skip_group_check=True)
                return ins
            add("pe", mm, reads=[("wm", s), "sc"], writes=["psM"], banks=[7])
        psM = ps[:, 7, 0:96].rearrange("p (j c) -> p j c", c=2)
        add("dve", lambda e: e.tensor_tensor(out=modv[:, :, 0], in0=psM[:, :, 0], in1=bm, op=ALU.add), reads=["psM", "bm"], writes=["modv0"], banks=[7])
        add("dve", lambda e: e.tensor_tensor(out=modv[:, :, 1], in0=psM[:, :, 1], in1=bm, op=ALU.add), reads=["psM", "bm"], writes=["modv1"], banks=[7])
        add("dve", lambda e: e.scalar_tensor_tensor(out=a_x, in0=modv[:, 16:32, 0], scalar=1.0, in1=ng, op0=ALU.add, op1=ALU.mult),
            reads=["modv0", "ng"], writes=["a_x"])
        add("dve", lambda e: e.scalar_tensor_tensor(out=a_c, in0=modv[:, 16:32, 1], scalar=1.0, in1=ng, op0=ALU.add, op1=ALU.mult),
            reads=["modv1", "ng"], writes=["a_c"])
        import os as _os
        if "gates" not in _os.environ.get("K_SKIP", ""):
            add("sp", lambda e: e.dma_start(out=gate_s.rearrange("(j p) -> p j", p=128), in_=modv[:, 32:48, 0],
                                            allow_slow_non_contiguous=True), reads=["modv0"], writes=["gate_s"], dma=True)
        add("dve", lambda e: e.tensor_tensor(out=lprod[:, 0, :], in0=lamv[:, 0, :], in1=lamv[:, 1, :], op=ALU.mult), reads=["lamv"], writes=["lp0"])
        add("dve", lambda e: e.tensor_tensor(out=lprod[:, 1, :], in0=lamv[:, 2, :], in1=lamv[:, 3, :], op=ALU.mult), reads=["lamv"], writes=["lp1"])
        add("dve", lambda e: e.reduce_sum(out=lsum, in_=lprod, axis=AX.X), reads=["lp0", "lp1"], writes=["lsum"])
        add("act", lambda e: e.activation(out=lsum, in_=lsum, func=AF.Exp), reads=["lsum"], writes=["lsum"])
        add("dve", lambda e: e.scalar_tensor_tensor(out=nlam, in0=lsum[:, 1:2], scalar=-LAMBDA_INIT, in1=lsum[:, 0:1], op0=ALU.add, op1=ALU.subtract),
            reads=["lsum"], writes=["nlam"])
        base_persist = A.off
    b_x = modv[:, 0:16, 0]
    b_c = modv[:, 0:16, 1]

    if "B" in phases:
        A.reset(base_persist)
        xt = [A.alloc([128, 2048], F32) for _ in range(2)]
        xs = [A.alloc([128, 2048], BF16) for _ in range(2)]
        junk = A.alloc([128, 2048], BF16)
        ssq = [A.alloc([128, 1], F32) for _ in range(2)]
        hxT = [A.alloc([128, 16, 512], BF16) for _ in range(2)]
        NW = 4
        wt = [A.alloc([128, 16, 512], BF16) for _ in range(NW)]
        ropeR = A.alloc([128, 2, 128], F32)
        ropeC = A.alloc([128, 2, 64], F32)
        cosT = A.alloc([128, 8, 64], F32)
        sinT = A.alloc([128, 8, 64], F32)
        subg = A.alloc([128, 1024], F32)
        NST = 6
        stg = [A.alloc([128, 512], BF16) for _ in range(NST)]
        q16 = [A.alloc([128, 512], BF16) for _ in range(2)]
        t1 = [A.alloc([128, 512], F32) for _ in range(2)]
        t2 = [A.alloc([128, 512], F32) for _ in range(2)]
        gt = [A.alloc([128, 512], F32) for _ in range(2)]
        add("sp", lambda e: e.dma_start(out=ropeR, in_=ropeR_d), writes=["ropeR"], dma=True)
        add("sp", lambda e: e.dma_start(out=ropeC, in_=ropeC_d), writes=["ropeC"], dma=True)
        add("sp", lambda e: e.dma_start(out=subg, in_=subg_d), writes=["subg"], dma=True)
        add("dve", lambda e: e.tensor_scalar(out=subg, in0=subg, scalar1=1.0 - LAMBDA_INIT, scalar2=None, op0=ALU.mult), reads=["subg"], writes=["subg"])

        st = {"x": 0, "stg": 0, "acc": 0, "r": 0, "w": 0, "k": 0, "ev": 0}
        cosT_f = cosT.rearrange("p a b -> p (a b)")
        sinT_f = sinT.rearrange("p a b -> p (a b)")

        def norm_sub(src, s, hx, hkey, av, bv):
            sl = st["x"] % 2
            st["x"] += 1
            add("sp", lambda e: e.dma_start(out=xt[sl], in_=src[s * 128:(s + 1) * 128, :]), writes=[("xt", sl)], dma=True)
            add("act", lambda e: e.activation(out=junk, in_=xt[sl], func=AF.Square, accum_out=ssq[sl]), reads=[("xt", sl)], writes=[("ssq", sl)])
            add("dve", lambda e: e.tensor_scalar(out=ssq[sl], in0=ssq[sl], scalar1=1.0 / 2048, scalar2=1e-6, op0=ALU.mult, op1=ALU.add),
                reads=[("ssq", sl)], writes=[("ssq", sl)])
            add("act", lambda e: e.activation(out=ssq[sl], in_=ssq[sl], func=AF.Ln), reads=[("ssq", sl)], writes=[("ssq", sl)])
            add("act", lambda e: e.activation(out=ssq[sl], in_=ssq[sl], func=AF.Exp, scale=-0.5), reads=[("ssq", sl)], writes=[("ssq", sl)])
            add("dve", lambda e: e.tensor_scalar(out=xs[sl], in0=xt[sl], scalar1=ssq[sl][:, 0:1], scalar2=None, op0=ALU.mult),
                reads=[("xt", sl), ("ssq", sl)], writes=[("xs", sl)])

            def tr(e):
                ins = None
                for kc in range(16):
                    ins = e.transpose(out=psb[:, kc // 8, (kc % 8) * 128:(kc % 8 + 1) * 128], in_=xs[sl][:, kc * 128:(kc + 1) * 128], identity=ident)
                return ins
            add("pe", tr, reads=[("xs", sl), "ident"], writes=["psX"], banks=[0, 1])
            for kc in range(16):
                src_ps = psb[:, kc // 8, (kc % 8) * 128:(kc % 8 + 1) * 128]
                dst = hxT[hx][:, kc, s * 128:(s + 1) * 128]
                if kc < 8:
                    add("act", lambda e, src_ps=src_ps, dst=dst, kc=kc: e.activation(out=dst, in_=src_ps, func=AF.Identity, scale=av[:, kc:kc + 1], bias=bv[:, kc:kc + 1]),
                        reads=["psX", "a_x", "a_c", "modv0", "modv1"], writes=[(hkey, s, kc)], banks=[kc // 8])
                else:
                    add("dve", lambda e, src_ps=src_ps, dst=dst, kc=kc: e.tensor_scalar(out=dst, in0=src_ps, scalar1=av[:, kc:kc + 1], scalar2=bv[:, kc:kc + 1], op0=ALU.mult, op1=ALU.add),
                        reads=["psX", "a_x", "a_c", "modv0", "modv1"], writes=[(hkey, s, kc)], banks=[kc // 8])

        def hx_reads(hkey, subs):
            return [(hkey, s, kc) for s in subs for kc in range(16)]

        wb_pending = []

        def load_w(c0, first):
            sl = st["w"] % NW
            st["w"] += 1
            blk = c0 // 512
            if first:
                src = win_d[:, c0:c0 + 512].rearrange("(kc p) n -> p kc n", p=128)
                add("pool", lambda e: e.dma_start(out=wt[sl], in_=src), writes=[("wt", sl)], dma=True)
                while wb_pending:
                    wb_pending.pop(0)()
                wb_pending.append(lambda: add("pool", lambda e: e.dma_start(out=winb_s[blk], in_=wt[sl]), reads=[("wt", sl)], writes=[("winb", blk)], dma=True))
            else:
                while wb_pending:
                    wb_pending.pop(0)()
                add("sp", lambda e: e.dma_start(out=wt[sl], in_=winb_s[blk]), reads=[("winb", blk)], writes=[("wt", sl)], dma=True)
            return sl

        def next_acc():
            b = 2 + st["acc"] % 3
            st["acc"] += 1
            return b

        def next_stg():
            k = st["stg"] % NST
            st["stg"] += 1
            return k

        pending = []

        def flush_pending():
            while pending:
                pending.pop(0)()

        def store(k, dst, T=512, rows=128):
            add("pool", lambda e: e.dma_start(out=dst, in_=stg[k][0:rows, 0:T]), reads=[("stg", k)], writes=[("scr", st["k"])], dma=True)
            st["k"] += 1

        def fm_block(wsl, hb, hx, hkey, T, dst, rope):
            while len(pending) > 2:
                pending.pop(0)()
            bank = next_acc()
            nsub = T // 128

            def mm(e):
                ins = None
                for kc in range(16):
                    ins = e.matmul(ps[:, bank, 0:T], lhsT=wt[wsl][:, kc, hb * 128:(hb + 1) * 128], rhs=hxT[hx][:, kc, 0:T], start=(kc == 0), stop=(kc == 15))
                return ins
            add("pe", mm, reads=[("wt", wsl)] + hx_reads(hkey, range(nsub)), writes=[("acc", bank)], banks=[bank])
            if not rope:
                def post():
                    k = next_stg()
                    add("act", lambda e: e.copy(out=stg[k][:, 0:T], in_=ps[:, bank, 0:T]), reads=[("acc", bank)], writes=[("stg", k)], banks=[bank])
                    store(k, dst, T)
                pending.append(post)
            else:
                r = st["r"] % 2
                st["r"] += 1
                rb = 5 + r

                def post():
                    k = next_stg()
                    add("act", lambda e: e.copy(out=q16[r][:, 0:T], in_=ps[:, bank, 0:T]), reads=[("acc", bank)], writes=[("q16", r)], banks=[bank])
                    add("pe", lambda e: e.matmul(ps[:, rb, 0:T], lhsT=perm, rhs=q16[r][:, 0:T], start=True, stop=True),
                        reads=[("q16", r), "perm"], writes=[("psR", rb)], banks=[rb])
                    add("dve", lambda e: e.tensor_tensor(out=t1[r][:, 0:T], in0=ps[:, bank, 0:T], in1=cosT_f[:, 0:T], op=ALU.mult),
                        reads=[("acc", bank), "cosT"], writes=[("t1", r)], banks=[bank])
                    add("dve", lambda e: e.tensor_tensor(out=t2[r][:, 0:T], in0=ps[:, rb, 0:T], in1=sinT_f[:, 0:T], op=ALU.mult),
                        reads=[("psR", rb), "sinT"], writes=[("t2", r)], banks=[rb])
                    add("pool", lambda e: e.tensor_tensor(out=stg[k][:, 0:T], in0=t1[r][:, 0:T], in1=t2[r][:, 0:T], op=ALU.add),
                        reads=[("t1", r), ("t2", r)], writes=[("stg", k)])
                    store(k, dst, T)
                pending.append(post)

        def tm_block(wsl, s, hx, hkey, dst, kind, c0g):
            while len(pending) > 2:
                pending.pop(0)()
            bank = next_acc()

            def mm(e):
                ins = None
                for kc in range(16):
                    ins = e.matmul(ps[:, bank, :], lhsT=hxT[hx][:, kc, s * 128:(s + 1) * 128], rhs=wt[wsl][:, kc, :], start=(kc == 0), stop=(kc == 15))
                return ins
            add("pe", mm, reads=[("wt", wsl)] + hx_reads(hkey, [s]), writes=[("acc", bank)], banks=[bank])

            def post():
                k = next_stg()
                if kind == "v":
                    add("dve", lambda e: e.tensor_copy(out=stg[k], in_=ps[:, bank, :]), reads=[("acc", bank)], writes=[("stg", k)], banks=[bank])
                elif kind == "gb":
                    add("act", lambda e: e.activation(out=stg[k], in_=ps[:, bank, :], func=AF.Silu), reads=[("acc", bank)], writes=[("stg", k)], banks=[bank])
                else:
                    g = st["ev"] % 2
                    st["ev"] += 1
                    add("act", lambda e: e.activation(out=gt[g], in_=ps[:, bank, :], func=AF.Silu), reads=[("acc", bank)], writes=[("gt", g)], banks=[bank])
                    add("pool", lambda e: e.tensor_tensor(out=stg[k], in0=gt[g], in1=subg[:, c0g:c0g + 512], op=ALU.mult),
                        reads=[("gt", g), "subg"], writes=[("stg", k)])
                store(k, dst)
            pending.append(post)

        def rope_tables(t):
            for r in range(8):
                rho = 8 * t + r
                add("dve", lambda e, r=r, rho=rho: e.tensor_scalar(out=cosT[:, r, :], in0=ropeC[:, 0, :], scalar1=ropeR[:, 0, rho:rho + 1], scalar2=None, op0=ALU.add),
                    reads=["ropeR", "ropeC"], writes=["cosT"])
                add("dve", lambda e, r=r, rho=rho: e.tensor_scalar(out=sinT[:, r, :], in0=ropeC[:, 1, :], scalar1=ropeR[:, 1, rho:rho + 1], scalar2=None, op0=ALU.add),
                    reads=["ropeR", "ropeC"], writes=["sinT"])

        tiles = []
        for t in range(16):
            tiles.append(("lat", t))
        tiles.append(("ctx", 0))
        import os as _os
        _kt = _os.environ.get("K_TILES")
        if _kt:
            tiles = [tiles[int(i)] for i in _kt.split(",")]
        _kb = _os.environ.get("K_BLOCKS")

        def tile_src(tl):
            if tl[0] == "lat":
                return x_d[tl[1] * 512:(tl[1] + 1) * 512, :], 4, a_x, b_x
            return ctx_d, 2, a_c, b_c

        def tile_blocks(tl):
            kind, t = tl
            if _kb:
                return [(int(c), None) for c in _kb.split(",")]
            if kind == "lat" and t < 8:
                return [(c, None) for c in range(0, 8192, 512)]
            blocks = [(1024, None), (1536, None), (2048, None), (2560, None)]
            if kind == "ctx" or t == 8:
                blocks += [(5120, None), (5632, None), (6144, None), (6656, None)]
            return blocks

        work = [(ti, c0) for ti, tl in enumerate(tiles) for (c0, _) in tile_blocks(tl)]
        wslots = {}
        wi = {"n": 0}

        def prefetch_w(upto):
            while wi["n"] < min(upto, len(work)):
                wslots[wi["n"]] = load_w(work[wi["n"]][1], work[wi["n"]][0] == 0)
                wi["n"] += 1
        widx = {"n": 0}
        src0, ns0, av0, bv0 = tile_src(tiles[0])
        for s in range(ns0):
            norm_sub(src0, s, 0, ("hx", 0), av0, bv0)
        for ti, tl in enumerate(tiles):
            hx = ti % 2
            hkey = ("hx", hx)
            kind, t = tl
            T = 512 if kind == "lat" else 256
            nsub = T // 128
            tok0 = t * 512 if kind == "lat" else NT_ALL
            if kind == "lat":
                rope_tables(t)
            blocks = tile_blocks(tl)
            nxt = tiles[ti + 1] if ti + 1 < len(tiles) else None
            nxt_subs = []
            if nxt is not None:
                srcn, nsn, avn, bvn = tile_src(nxt)
                nxt_subs = list(range(nsn))
            step = max(1, len(blocks) // (len(nxt_subs) + 1)) if nxt_subs else 0
            for bi, (c0, _) in enumerate(blocks):
                prefetch_w(widx["n"] + 3)
                wsl = wslots[widx["n"]]
                widx["n"] += 1
                grp = c0 // 1024
                hb0 = (c0 % 1024) // 128
                is_own = (kind == "lat" and t < 8)
                if grp in (0, 1, 4, 5):
                    for hb in range(4):
                        h = hb0 + hb
                        if grp == 0:
                            fm_block(wsl, hb, hx, hkey, T, qaT_s[h][:, tok0:tok0 + T], True)
                        elif grp == 1:
                            fm_block(wsl, hb, hx, hkey, T, kaT_s[h][:, tok0:tok0 + T], kind == "lat")
                        elif grp == 4:
                            fm_block(wsl, hb, hx, hkey, T, qbT_s[h][:, tok0:tok0 + T], False)
                        else:
                            if is_own:
                                fm_block(wsl, hb, hx, hkey, T, kbT_s[h][:, tok0:tok0 + T], False)
                            elif kind == "lat":
                                fm_block(wsl, hb, hx, hkey, 256, kbT_s[h][:, NT_OWN:NT_OWN + 256], False)
                            else:
                                fm_block(wsl, hb, hx, hkey, 256, kbT_s[h][:, NT_OWN + 256:NT_OWN + 512], False)
                else:
                    cc = c0 % 1024
                    for s in range(nsub):
                        if grp == 2:
                            tm_block(wsl, s, hx, hkey, vA_s[tok0 + s * 128:tok0 + (s + 1) * 128, cc:cc + 512], "v", cc)
                        elif grp == 3:
                            tm_block(wsl, s, hx, hkey, gA_s[tok0 + s * 128:tok0 + (s + 1) * 128, cc:cc + 512], "ga", cc)
                        elif grp == 7:
                            tm_block(wsl, s, hx, hkey, gB_s[tok0 + s * 128:tok0 + (s + 1) * 128, cc:cc + 512], "gb", cc)
                        else:
                            if is_own:
                                tm_block(wsl, s, hx, hkey, vB_s[tok0 + s * 128:tok0 + (s + 1) * 128, cc:cc + 512], "v", cc)
                            elif kind == "lat":
                                if s < 2:
                                    tm_block(wsl, s, hx, hkey, vB_s[NT_OWN + s * 128:NT_OWN + (s + 1) * 128, cc:cc + 512], "v", cc)
                            else:
                                tm_block(wsl, s, hx, hkey, vB_s[NT_OWN + 256 + s * 128:NT_OWN + 256 + (s + 1) * 128, cc:cc + 512], "v", cc)
                if nxt_subs and (bi + 1) % step == 0:
                    s = nxt_subs.pop(0)
                    norm_sub(srcn, s, 1 - hx, ("hx", 1 - hx), avn, bvn)
            flush_pending()
            while nxt_subs:
                s = nxt_subs.pop(0)
                norm_sub(srcn, s, 1 - hx, ("hx", 1 - hx), avn, bvn)
        Pg.barrier()

    if "C" in phases:
        A.reset(base_persist)
        kaT = [A.alloc([128, NKA], BF16) for _ in range(2)]
        vaA = [A.alloc([128, 66, 129], BF16) for _ in range(2)]
        qt = [A.alloc([128, 512], BF16) for _ in range(2)]
        Gt = [A.alloc([128, 4, 128], BF16) for _ in range(2)]
        NP = 3
        Pt = [A.alloc([128, 2, 512], BF16) for _ in range(NP)]
        rl = A.alloc([128, 4, 2], F32)
        nl2 = A.alloc([128, 4], F32)
        t1c = A.alloc([128, 128], F32)
        Dd = A.alloc([128, 4, 128], F32)
        sqj = A.alloc([128, 128], F32)
        ssq4 = A.alloc([128, 4], F32)
        mst = [A.alloc([128, 4, 128], BF16) for _ in range(2)]
        for i in range(2):
            add("pool", lambda e, i=i: e.memset(vaA[i][:, :, 128:129], 1.0), writes=[("va1", i)])
        psS = [ps[:, 0:2, :], ps[:, 2:4, :]]

        def ogrp(g):
            return ps[:, 4 + g // 3, (g % 3) * 129:(g % 3) * 129 + 129]
        cnt = {"s": 0, "p": 0, "q": 0}
        for h in range(8):
            hs = h % 2
            add("sp", lambda e, h=h, hs=hs: e.dma_start(out=kaT[hs], in_=kaT_s[h]), writes=[("kaT", hs)], dma=True)
            add("sp", lambda e, h=h, hs=hs: e.dma_start(out=vaA[hs][:, :, 0:128], in_=vA_s[:, h * 128:(h + 1) * 128].rearrange("(c p) d -> p c d", p=128)),
                writes=[("vaA", hs)], dma=True)
            for qb in range(8):
                qs = cnt["q"] % 2
                cnt["q"] += 1
                add("sp", lambda e, h=h, qb=qb, qs=qs: e.dma_start(out=qt[qs], in_=qaT_s[h][:, qb * 512:(qb + 1) * 512]), writes=[("qt", qs)], dma=True)
                add("sp", lambda e, h=h, qb=qb, qs=qs: e.dma_start(out=Gt[qs], in_=gA_s[qb * 512:(qb + 1) * 512, h * 128:(h + 1) * 128].rearrange("(s p) d -> p s d", p=128)),
                    writes=[("Gt", qs)], dma=True)

                def S_op(kc, hs=hs, qs=qs):
                    sl = cnt["s"] % 2
                    cnt["s"] += 1

                    def f(e):
                        e.matmul(psS[sl][:, 0, :], lhsT=kaT[hs][0:64, kc * 128:(kc + 1) * 128], rhs=qt[qs][0:64, :], start=True, stop=True)
                        return e.matmul(psS[sl][:, 1, :], lhsT=kaT[hs][64:128, kc * 128:(kc + 1) * 128], rhs=qt[qs][64:128, :], start=True, stop=True)
                    add("pe", f, reads=[("kaT", hs), ("qt", qs)], writes=[("S", sl)], banks=[2 * sl, 2 * sl + 1])
                    return sl
                sls = {0: S_op(0), 1: S_op(1)}
                for kc in range(66):
                    sl = sls[kc]
                    pl = cnt["p"] % NP
                    cnt["p"] += 1
                    add("act", lambda e, sl=sl, pl=pl: e.activation(out=Pt[pl], in_=psS[sl], func=AF.Exp, scale=QSCALE_A),
                        reads=[("S", sl)], writes=[("P", pl)], banks=[2 * sl, 2 * sl + 1])
                    if kc + 2 < 66:
                        sls[kc + 2] = S_op(kc + 2)

                    def pv(e, kc=kc, pl=pl, hs=hs):
                        ins = None
                        for m in range(2):
                            for s in range(4):
                                g = m * 4 + s
                                ins = e.matmul(ogrp(g), lhsT=Pt[pl][:, m, s * 128:(s + 1) * 128], rhs=vaA[hs][:, kc, :],
                                               start=(kc == 0 and g % 3 == 0), stop=(kc == 65), skip_group_check=True)
                        return ins
                    add("pe", pv, reads=[("P", pl), ("vaA", hs), ("va1", hs)], writes=["O"], banks=[4, 5, 6])
                ms = (h * 8 + qb) % 2
                for s in range(4):
                    o1 = ogrp(s)
                    o2 = ogrp(4 + s)
                    add("dve", lambda e, s=s, o1=o1: e.reciprocal(out=rl[:, s, 0:1], in_=o1[:, 128:129]), reads=["O"], writes=[("rl", s)], banks=[4, 5, 6])
                    add("dve", lambda e, s=s, o2=o2: e.reciprocal(out=rl[:, s, 1:2], in_=o2[:, 128:129]), reads=["O"], writes=[("rl2", s)], banks=[4, 5, 6])
                    add("dve", lambda e, s=s: e.tensor_scalar(out=nl2[:, s:s + 1], in0=rl[:, s, 1:2], scalar1=nlam[:, 0:1], scalar2=None, op0=ALU.mult),
                        reads=[("rl2", s), "nlam"], writes=[("nl2", s)])
                    add("dve", lambda e, s=s, o1=o1: e.tensor_scalar(out=t1c, in0=o1[:, 0:128], scalar1=rl[:, s, 0:1], scalar2=None, op0=ALU.mult),
                        reads=["O", ("rl", s)], writes=["t1c"], banks=[4, 5, 6])
                    add("dve", lambda e, s=s, o2=o2: e.scalar_tensor_tensor(out=Dd[:, s, :], in0=o2[:, 0:128], scalar=nl2[:, s:s + 1], in1=t1c, op0=ALU.mult, op1=ALU.add),
                        reads=["O", ("nl2", s), "t1c"], writes=[("D", s)], banks=[4, 5, 6])
                    add("dve", lambda e, s=s: e.scalar_tensor_tensor(out=sqj, in0=Dd[:, s, :], scalar=1.0, in1=Dd[:, s, :], op0=ALU.mult, op1=ALU.mult, accum_out=ssq4[:, s:s + 1]),
                        reads=[("D", s)], writes=["ssq4"])
                add("dve", lambda e: e.tensor_scalar(out=ssq4, in0=ssq4, scalar1=1.0 / 128, scalar2=1e-5, op0=ALU.mult, op1=ALU.add),
                    reads=["ssq4"], writes=["ssq4"])
                add("act", lambda e: e.activation(out=ssq4, in_=ssq4, func=AF.Ln), reads=["ssq4"], writes=["ssq4"])
                add("act", lambda e: e.activation(out=ssq4, in_=ssq4, func=AF.Exp, scale=-0.5), reads=["ssq4"], writes=["ssq4"])
                for s in range(4):
                    add("dve", lambda e, s=s, ms=ms, qs=qs: e.scalar_tensor_tensor(out=mst[ms][:, s, :], in0=Dd[:, s, :], scalar=ssq4[:, s:s + 1], in1=Gt[qs][:, s, :], op0=ALU.mult, op1=ALU.mult),
                        reads=[("D", s), "ssq4", ("Gt", qs)], writes=[("mst", ms, s)])
                add("pool", lambda e, h=h, qb=qb, ms=ms: e.dma_start(out=mixed_s[qb * 512:(qb + 1) * 512, h * 128:(h + 1) * 128].rearrange("(s p) d -> p s d", p=128), in_=mst[ms]),
                    reads=[("mst", ms, s) for s in range(4)], writes=[("mixA", h, qb)], dma=True)
        Pg.barrier()

    if "D" in phases:
        A.reset(base_persist)
        kbT = [A.alloc([128, NKB], BF16) for _ in range(2)]
        vbB = [A.alloc([128, 36, 129], BF16) for _ in range(2)]
        nabt = [A.alloc([128, 3, 640], F32) for _ in range(2)]
        namt = A.alloc([128, 3, 640], F32)
        qtb = [A.alloc([128, 512], BF16) for _ in range(2)]
        Gtb = [A.alloc([128, 4, 128], BF16) for _ in range(2)]
        tmpb = [A.alloc([128, 640], F32) for _ in range(2)]
        Ptb = [A.alloc([128, 896], BF16) for _ in range(3)]
        rlb = A.alloc([128, 4], F32)
        t1b = A.alloc([128, 128], F32)
        mstb = [A.alloc([128, 4, 128], BF16) for _ in range(2)]
        for i in range(2):
            add("pool", lambda e, i=i: e.memset(vbB[i][:, :, 128:129], 1.0), writes=[("vb1", i)])
        add("sp", lambda e: e.dma_start(out=namt, in_=nam_d), writes=["namt"], dma=True)
        psSb = [ps[:, 0:2, :].rearrange("p a b -> p (a b)"), ps[:, 2:4, :].rearrange("p a b -> p (a b)")]

        def ogrpb(g):
            return ps[:, 4 + g // 3, (g % 3) * 129:(g % 3) * 129 + 129]
        cn = {"s": 0, "p": 0, "q": 0}
        for h in range(8):
            hs = h % 2
            add("sp", lambda e, h=h, hs=hs: e.dma_start(out=kbT[hs], in_=kbT_s[h]), writes=[("kbT", hs)], dma=True)
            add("sp", lambda e, h=h, hs=hs: e.dma_start(out=vbB[hs][:, :, 0:128], in_=vB_s[:, h * 128:(h + 1) * 128].rearrange("(c p) d -> p c d", p=128)),
                writes=[("vbB", hs)], dma=True)
            add("sp", lambda e, h=h, hs=hs: e.dma_start(out=nabt[hs], in_=nab_d[h]), writes=[("nab", hs)], dma=True)
            add("dve", lambda e, hs=hs: e.tensor_tensor(out=nabt[hs], in0=nabt[hs], in1=namt, op=ALU.add), reads=[("nab", hs), "namt"], writes=[("nab", hs)])
            qslot = {}

            def loads(qg, h=h):
                qs = cn["q"] % 2
                cn["q"] += 1
                qslot[qg] = qs
                add("sp", lambda e: e.dma_start(out=qtb[qs], in_=qbT_s[h][:, qg * 512:(qg + 1) * 512]), writes=[("qtb", qs)], dma=True)
                add("sp", lambda e: e.dma_start(out=Gtb[qs], in_=gB_s[qg * 512:(qg + 1) * 512, h * 128:(h + 1) * 128].rearrange("(s p) d -> p s d", p=128)),
                    writes=[("Gtb", qs)], dma=True)
            info = {}

            def issue_S(i, hs=hs):
                qg, s = i // 4, i % 4
                if qg not in qslot:
                    loads(qg)
                qs = qslot[qg]
                tq = i
                cs = max(tq - 2, 0)
                pat = tq if tq < 2 else 2
                sl = cn["s"] % 2
                cn["s"] += 1
                chunks = [cs + ci for ci in range(5)] + [34, 35]

                def sm(e):
                    ins = None
                    for c, ch in enumerate(chunks):
                        ins = e.matmul(psSb[sl][:, c * 128:(c + 1) * 128], lhsT=kbT[hs][:, ch * 128:(ch + 1) * 128], rhs=qtb[qs][:, s * 128:(s + 1) * 128], start=True, stop=True)
                    return ins
                add("pe", sm, reads=[("kbT", hs), ("qtb", qs)], writes=[("Sb", sl)], banks=[2 * sl, 2 * sl + 1])
                info[i] = (sl, chunks, pat)
            issue_S(0)
            issue_S(1)
            for i in range(32):
                qg, s = i // 4, i % 4
                qs = qslot[qg]
                sl, chunks, pat = info[i]
                ms = (h * 8 + qg) % 2
                tb = cn["p"] % 2
                pl = cn["p"] % 3
                cn["p"] += 1
                add("dve", lambda e, sl=sl, tb=tb, hs=hs, pat=pat: e.scalar_tensor_tensor(out=tmpb[tb], in0=psSb[sl][:, 0:640], scalar=QSCALE_B, in1=nabt[hs][:, pat, :], op0=ALU.mult, op1=ALU.add),
                    reads=[("Sb", sl), ("nab", hs)], writes=[("tmpb", tb)], banks=[2 * sl, 2 * sl + 1])
                add("act", lambda e, sl=sl, pl=pl: e.activation(out=Ptb[pl][:, 640:896], in_=psSb[sl][:, 640:896], func=AF.Exp, scale=QSCALE_B), reads=[("Sb", sl)], writes=[("Pbc", pl)], banks=[2 * sl + 1])
                add("act", lambda e, tb=tb, pl=pl: e.activation(out=Ptb[pl][:, 0:640], in_=tmpb[tb], func=AF.Exp), reads=[("tmpb", tb)], writes=[("Pbw", pl)])
                if i + 2 < 32:
                    issue_S(i + 2)

                def pvb(e, pl=pl, chunks=chunks, hs=hs, s=s):
                    ins = None
                    for c, ch in enumerate(chunks):
                        ins = e.matmul(ogrpb(s), lhsT=Ptb[pl][:, c * 128:(c + 1) * 128], rhs=vbB[hs][:, ch, :], start=(c == 0 and s % 3 == 0), stop=(c == 6), skip_group_check=True)
                    return ins
                add("pe", pvb, reads=[("Pbw", pl), ("Pbc", pl), ("vbB", hs), ("vb1", hs)], writes=[("Ob", s)], banks=[4 + s // 3])
                if s == 3:
                    for s2 in range(4):
                        ob = ogrpb(s2)
                        add("dve", lambda e, s2=s2, ob=ob: e.reciprocal(out=rlb[:, s2:s2 + 1], in_=ob[:, 128:129]), reads=[("Ob", s2)], writes=[("rlb", s2)], banks=[4 + s2 // 3])
                        add("dve", lambda e, s2=s2, ob=ob, ms=ms, qs=qs: e.scalar_tensor_tensor(out=mstb[ms][:, s2, :], in0=ob[:, 0:128], scalar=rlb[:, s2:s2 + 1], in1=Gtb[qs][:, s2, :], op0=ALU.mult, op1=ALU.mult),
                            reads=[("Ob", s2), ("rlb", s2), ("Gtb", qs)], writes=[("mstb", ms, s2)], banks=[4 + s2 // 3])
                    add("pool", lambda e, h=h, qg=qg, ms=ms: e.dma_start(out=mixed_s[qg * 512:(qg + 1) * 512, 1024 + h * 128:1024 + (h + 1) * 128].rearrange("(s p) d -> p s d", p=128), in_=mstb[ms]),
                        reads=[("mstb", ms, s2) for s2 in range(4)], writes=[("mixB", h, qg)], dma=True)
        Pg.barrier()

    if "E" in phases:
        A.reset(base_persist)
        wo = A.alloc([128, 16, 2048], BF16)
        gbc = A.alloc([128, 2048], F32)
        fgs = A.alloc([128, 2048], F32)
        mt = [A.alloc([128, 2048], BF16) for _ in range(2)]
        mT = [A.alloc([128, 16, 128], BF16) for _ in range(2)]
        xe = [A.alloc([128, 2048], F32) for _ in range(2)]
        hsb = [A.alloc([128, 2048], F32) for _ in range(2)]
        ob_ = [A.alloc([128, 2048], F32) for _ in range(2)]
        junk2 = A.alloc([128, 2048], BF16)
        sse = [A.alloc([128, 1], F32) for _ in range(2)]
        for blk in range(4):
            add("pool", lambda e, blk=blk: e.dma_start(out=wo[:, :, blk * 512:(blk + 1) * 512], in_=wout_d[:, blk * 512:(blk + 1) * 512].rearrange("(kc p) n -> p kc n", p=128)), writes=[("wo", blk)], dma=True)
        add("sp", lambda e: e.dma_start(out=gbc, in_=gate_s.partition_broadcast(128)), reads=["gate_s"], writes=["gbc"], dma=True)
        add("sp", lambda e: e.dma_start(out=fgs, in_=fgb_d), writes=["fgs"], dma=True)
        psY = ps[:, 4:8, :].rearrange("p a b -> p (a b)")
        for ti in range(32):
            sl = ti % 2
            add("sp", lambda e, ti=ti, sl=sl: e.dma_start(out=mt[sl], in_=mixed_s[ti * 128:(ti + 1) * 128, :]), writes=[("mt", sl)], dma=True)
            add("sp", lambda e, ti=ti, sl=sl: e.dma_start(out=xe[sl], in_=x_d[ti * 128:(ti + 1) * 128, :]), writes=[("xe", sl)], dma=True)

            def tr(e, sl=sl):
                ins = None
                for kc in range(16):
                    ins = e.transpose(out=psb[:, kc // 8, (kc % 8) * 128:(kc % 8 + 1) * 128], in_=mt[sl][:, kc * 128:(kc + 1) * 128], identity=ident)
                return ins
            add("pe", tr, reads=[("mt", sl), "ident"], writes=["psXe"], banks=[0, 1])
            add("act", lambda e, sl=sl: e.copy(out=mT[sl][:, 0:8, :], in_=psb[:, 0, :].rearrange("p (a b) -> p a b", b=128)), reads=["psXe"], writes=[("mT0", sl)], banks=[0])
            add("dve", lambda e, sl=sl: e.tensor_copy(out=mT[sl][:, 8:16, :], in_=psb[:, 1, :].rearrange("p (a b) -> p a b", b=128)), reads=["psXe"], writes=[("mT1", sl)], banks=[1])

            for hf in range(2):
                def mm(e, sl=sl, hf=hf):
                    ins = None
                    for nb in range(2 * hf, 2 * hf + 2):
                        for kc in range(16):
                            ins = e.matmul(ps[:, 4 + nb, :], lhsT=mT[sl][:, kc, :], rhs=wo[:, kc, nb * 512:(nb + 1) * 512], start=(kc == 0), stop=(kc == 15))
                    return ins
                add("pe", mm, reads=[("mT0", sl), ("mT1", sl)] + [("wo", b) for b in range(4)], writes=[("psY", hf)], banks=[4 + 2 * hf, 5 + 2 * hf])
                psYh = ps[:, 4 + 2 * hf:6 + 2 * hf, :].rearrange("p a b -> p (a b)")
                add("dve", lambda e, sl=sl, hf=hf, psYh=psYh: e.tensor_tensor(out=hsb[sl][:, hf * 1024:(hf + 1) * 1024], in0=psYh, in1=gbc[:, hf * 1024:(hf + 1) * 1024], op=ALU.mult),
                    reads=[("psY", hf), "gbc"], writes=[("hs", sl, hf)], banks=[4 + 2 * hf, 5 + 2 * hf])
                add("pool", lambda e, sl=sl, hf=hf: e.tensor_tensor(out=hsb[sl][:, hf * 1024:(hf + 1) * 1024], in0=hsb[sl][:, hf * 1024:(hf + 1) * 1024], in1=xe[sl][:, hf * 1024:(hf + 1) * 1024], op=ALU.add),
                    reads=[("hs", sl, hf), ("xe", sl)], writes=[("hs", sl, hf)])
            add("act", lambda e, sl=sl: e.activation(out=junk2, in_=hsb[sl], func=AF.Square, accum_out=sse[sl]), reads=[("hs", sl, 0), ("hs", sl, 1)], writes=[("sse", sl)])
            add("dve", lambda e, sl=sl: e.tensor_scalar(out=sse[sl], in0=sse[sl], scalar1=1.0 / 2048, scalar2=1e-6, op0=ALU.mult, op1=ALU.add), reads=[("sse", sl)], writes=[("sse", sl)])
            add("act", lambda e, sl=sl: e.activation(out=sse[sl], in_=sse[sl], func=AF.Ln), reads=[("sse", sl)], writes=[("sse", sl)])
            add("act", lambda e, sl=sl: e.activation(out=sse[sl], in_=sse[sl], func=AF.Exp, scale=-0.5), reads=[("sse", sl)], writes=[("sse", sl)])
            add("dve", lambda e, sl=sl: e.scalar_tensor_tensor(out=ob_[sl], in0=hsb[sl], scalar=sse[sl][:, 0:1], in1=fgs, op0=ALU.mult, op1=ALU.mult),
                reads=[("hs", sl, 0), ("hs", sl, 1), ("sse", sl), "fgs"], writes=[("ob", sl)])
            add("pool", lambda e, ti=ti, sl=sl: e.dma_start(out=out_d[ti * 128:(ti + 1) * 128, :], in_=ob_[sl]), reads=[("ob", sl)], writes=[("out", ti)], dma=True)
        Pg.barrier()

    if Pg.limit is not None:
        print("ops recorded", len(Pg.ops), [(o.eng, o.oid) for o in Pg.ops[-3:]])
    Pg.limit = None
    Pg.barrier()
    add("sp", lambda e: e.wait_ge(Pg.sems[("dma", ("sp", 0))], 16))
    Pg.finalize(es)
    with nc.Block() as block:
        Pg.emit(block)
    es.close()
    return nc


def _grow(half, rho):
    return rho if half == 0 else 127 - rho


def _rope_tables(half):
    f32 = np.float32
    inv = (np.float32(10000.0) ** (-np.arange(16, dtype=f32) / np.float32(16))).astype(f32)
    R = np.zeros((128, 2, 128), f32)
    C = np.zeros((128, 2, 64), f32)
    rows = np.array([_grow(half, r) for r in range(128)], f32)
    cols = np.arange(64, dtype=f32)
    for p in range(128):
        d = p % 64
        fr = inv[p % 16]
        sgn = f32(-1.0) if (p % 32) < 16 else f32(1.0)
        if d < 32:
            ang = (rows * fr).astype(f32)
            R[p, 0] = np.cos(ang).astype(f32)
            R[p, 1] = sgn * np.sin(ang).astype(f32)
        else:
            ang = (cols * fr).astype(f32)
            C[p, 0] = np.cos(ang).astype(f32)
            C[p, 1] = sgn * np.sin(ang).astype(f32)
    return R, C


def _perm_matrix():
    Pm = np.zeros((128, 128), np.float32)
    for p in range(128):
        partner = p + 16 if (p % 32) < 16 else p - 16
        Pm[partner, p] = 1.0
    return Pm


def _na_tables(half, rpb):
    nab = np.zeros((8, 128, 3, 640), np.float32)
    nam = np.full((128, 3, 640), NEG, np.float32)
    j = np.arange(64)
    c0 = np.clip(j - 8, 0, 48)
    kc = np.arange(64)
    incol = (kc[:, None] >= c0[None, :]) & (kc[:, None] <= c0[None, :] + 15)
    coff = kc[:, None] - j[None, :] + 15
    coff_c = np.clip(coff, 0, 30)
    for pi, tq in enumerate((0, 1, 2)):
        cs = max(tq - 2, 0)
        for ci in range(5):
            lc = cs + ci
            for a in range(2):
                kr = _grow(half, 2 * lc + a)
                for dr in range(2):
                    r = _grow(half, 2 * tq + dr)
                    r0 = min(max(r - 4, 0), 120)
                    if not (r0 <= kr <= r0 + 7):
                        continue
                    roff = kr - r + 7
                    sl = slice(ci * 128 + dr * 64, ci * 128 + dr * 64 + 64)
                    nam[a * 64:(a + 1) * 64, pi, sl] = np.where(incol, 0.0, NEG)
                    vals = rpb[:, roff, :][:, coff_c]
                    nab[:, a * 64:(a + 1) * 64, pi, sl] = np.where(incol[None], vals, 0.0)
    return nab, nam


def _core_inputs(b, half, inp, shared):
    f32 = np.float32
    x = inp["x"]
    rows = [_grow(half, r) for r in range(128)]
    x_loc = np.ascontiguousarray(x[b].reshape(128, 64, 2048)[rows].reshape(8192, 2048))
    cvec = np.stack([inp["c"][b].reshape(16, 128).T, inp["c_ctx"].reshape(16, 128).T], axis=-1)
    R, C = shared["rope"][half]
    nab, nam = shared["na"][half]
    return {
        "x_loc": x_loc,
        "ctx": np.ascontiguousarray(inp["ctx"][b]),
        "cvec": np.ascontiguousarray(cvec.astype(f32)),
        "w_mod": shared["w_mod"], "b_mod2": shared["b_mod2"], "norm_g2": shared["norm_g2"],
        "w_in": shared["w_in"], "w_out": shared["w_out"], "lamv": shared["lamv"],
        "sublng": shared["sublng"], "fgb": shared["fgb"],
        "ropeR": R, "ropeC": C, "permm": shared["perm"], "ident": shared["ident"],
        "nab": nab, "nam": nam,
    }


def _shared(inp):
    f32 = np.float32
    sh = {}
    sh["w_mod"] = np.ascontiguousarray(inp["w_mod"][0], dtype=f32)
    sh["b_mod2"] = np.ascontiguousarray(inp["b_mod"][0].reshape(48, 128).T, dtype=f32)
    sh["norm_g2"] = np.ascontiguousarray(inp["norm_g"][0].reshape(16, 128).T, dtype=f32)
    sh["w_in"] = np.ascontiguousarray(inp["w_in"][0], dtype=f32)
    sh["w_out"] = np.ascontiguousarray(inp["w_out"][0], dtype=f32)
    lam = np.stack([inp["lam_q1"][0], inp["lam_k1"][0], inp["lam_q2"][0], inp["lam_k2"][0]], 0)
    sh["lamv"] = np.ascontiguousarray(np.broadcast_to(lam[None], (128, 4, 64)), dtype=f32)
    sh["sublng"] = np.ascontiguousarray(np.broadcast_to(np.tile(inp["subln_g"][0], 8)[None], (128, 1024)), dtype=f32)
    sh["fgb"] = np.ascontiguousarray(np.broadcast_to(inp["final_g"][None], (128, 2048)), dtype=f32)
    sh["perm"] = _perm_matrix()
    sh["ident"] = np.eye(128, dtype=f32)
    sh["rope"] = [_rope_tables(h) for h in range(2)]
    rpb = np.asarray(inp["rpb"][0], dtype=f32)
    sh["na"] = [_na_tables(h, rpb) for h in range(2)]
    return sh


_NC_CACHE = {}


def kernel(**inputs):
    inp = {k: np.asarray(v) for k, v in inputs.items()}
    sh = _shared(inp)
    in_maps = []
    for core in range(8):
        b, half = core // 2, core % 2
        in_maps.append(_core_inputs(b, half, inp, sh))
    if "nc" not in _NC_CACHE:
        _NC_CACHE["nc"] = build_nc()
    nc = _NC_CACHE["nc"]
    res = run_bass_kernel_spmd(nc, in_maps, core_ids=list(range(8)))
    out = np.empty((4, 8192, 2048), np.float32)
    for core in range(8):
        b, half = core // 2, core % 2
        o = np.asarray(res.results[core]["out_loc"]).reshape(64, 64, 2048)
        rows = [_grow(half, r) for r in range(64)]
        out[b].reshape(128, 64, 2048)[rows] = o
    return out
```

```python
import numpy as np
from concourse.bass_utils import run_bass_kernel_spmd
from contextlib import ExitStack
import concourse.bass as bass
import concourse.mybir as mybir

F32 = mybir.dt.float32
BF16 = mybir.dt.bfloat16
U8 = mybir.dt.uint8
AF = mybir.ActivationFunctionType
ALU = mybir.AluOpType
AX = mybir.AxisListType


class _Op:
    __slots__ = ("eng", "fn", "deps", "dma", "signal", "ev", "oid")

    def __init__(self, eng, fn, deps, dma, oid):
        self.eng = eng
        self.fn = fn
        self.deps = deps
        self.dma = dma
        self.signal = False
        self.ev = None
        self.oid = oid


class Prog:
    COMPUTE = ("pe", "act", "dve", "pool")
    EPOCH = 30000
    NDMA = 8

    def __init__(self, nc, same_engine_sync=True):
        self.nc = nc
        self.ops = []
        self.res_w = {}
        self.res_r = {}
        self.same = same_engine_sync
        self.dma_cnt = {}
        self.dma_last = {}
        self.barrier_deps = set()
        self.last_op = {}
        self.all_dma = []
        self.bank_last = {}

    limit = None

    def add(self, eng, fn, reads=(), writes=(), dma=False, banks=()):
        oid = len(self.ops)
        if self.limit is not None and oid >= self.limit:
            return None
        deps = set(self.barrier_deps)
        for b in banks:
            last = self.bank_last.get(b)
            if last is not None and self.ops[last].eng != eng:
                deps.add(last)
            self.bank_last[b] = oid
        for r in reads:
            w = self.res_w.get(r)
            if w is not None:
                deps.add(w)
        for wr in writes:
            w = self.res_w.get(wr)
            if w is not None:
                deps.add(w)
            for rd in self.res_r.get(wr, ()):
                deps.add(rd)
        if dma:
            i = self.dma_cnt.get(eng, 0)
            self.dma_cnt[eng] = i + 1
            key = (eng, i % self.NDMA)
            prev = self.dma_last.get(key)
            if prev is not None:
                deps.add(prev)
            self.dma_last[key] = oid
        op = _Op(eng, fn, deps, dma, oid)
        if self.limit is not None:
            print("OP", oid, eng, "dma" if dma else "", list(writes)[:2])
        if dma:
            op.ev = ("dma", key, 16 * (i // self.NDMA + 1))
            op.signal = True
            self.all_dma.append(oid)
        self.ops.append(op)
        for r in reads:
            self.res_r.setdefault(r, []).append(oid)
        for wr in writes:
            self.res_w[wr] = oid
            self.res_r[wr] = []
        self.last_op[eng] = oid
        return oid

    def barrier(self):
        deps = set(self.dma_last.values())
        for e, o in self.last_op.items():
            if not self.ops[o].dma:
                deps.add(o)
        self.barrier_deps = deps

    def _needs_sync(self, op, dep):
        if dep.dma:
            return True
        if dep.eng != op.eng:
            return True
        if op.dma:
            return True
        if op.eng == "pe":
            return False
        return self.same

    def finalize(self, es):
        nc = self.nc
        ops = self.ops
        for op in ops:
            for d in op.deps:
                dep = ops[d]
                if not dep.dma and self._needs_sync(op, dep):
                    dep.signal = True
        cnt = {e: 0 for e in self.COMPUTE}
        for op in ops:
            if op.dma or not op.signal:
                continue
            c = cnt[op.eng]
            op.ev = ("eng", op.eng, c // self.EPOCH, c % self.EPOCH + 1)
            cnt[op.eng] = c + 1
        self.sems = {}
        for e in self.COMPUTE:
            n_ep = (cnt[e] + self.EPOCH - 1) // self.EPOCH
            for k in range(max(n_ep, 1)):
                self.sems[("eng", e, k)] = es.enter_context(nc.semaphore(f"s_{e}_{k}"))
        for q, n in self.dma_cnt.items():
            for k in range(min(n, self.NDMA)):
                self.sems[("dma", (q, k))] = es.enter_context(nc.semaphore(f"d_{q}_{k}"))
        self.n_signal = cnt

    def emit_engine(self, eng, e):
        ops = self.ops
        seen_eng = {}
        seen_dma = {}
        for op in ops:
            if op.eng != eng:
                continue
            need_eng = {}
            need_dma = {}
            for d in op.deps:
                dep = ops[d]
                if not self._needs_sync(op, dep):
                    continue
                ev = dep.ev
                if ev[0] == "dma":
                    if need_dma.get(ev[1], 0) < ev[2]:
                        need_dma[ev[1]] = ev[2]
                else:
                    cur = need_eng.get(ev[1])
                    if cur is None or cur < (ev[2], ev[3]):
                        need_eng[ev[1]] = (ev[2], ev[3])
            for k, v in need_dma.items():
                if seen_dma.get(k, 0) < v:
                    e.wait_ge(self.sems[("dma", k)], v)
                    seen_dma[k] = v
            for k, v in need_eng.items():
                cur = seen_eng.get(k)
                if cur is None or cur < v:
                    e.wait_ge(self.sems[("eng", k, v[0])], v[1])
                    seen_eng[k] = v
            ins = op.fn(e)
            if op.signal:
                if op.dma:
                    ins.then_inc(self.sems[("dma", op.ev[1])], 16)
                else:
                    ins.then_inc(self.sems[("eng", op.eng, op.ev[2])], 1)

    def emit(self, block):
        P = self

        @block.sync
        def _(e):
            P.emit_engine("sp", e)

        @block.tensor
        def _(e):
            P.emit_engine("pe", e)

        @block.scalar
        def _(e):
            P.emit_engine("act", e)

        @block.vector
        def _(e):
            P.emit_engine("dve", e)

        @block.gpsimd
        def _(e):
            P.emit_engine("pool", e)


class Arena:
    def __init__(self, tensor, nbytes):
        self.t = tensor
        self.n = nbytes
        self.off = 0

    def reset(self, off=0):
        self.off = off

    def alloc(self, shape, dt):
        esz = mybir.dt.size(dt)
        free = int(np.prod(shape[1:]))
        nb = free * esz
        a = (self.off + 63) // 64 * 64
        assert a + nb <= self.n, f"arena overflow: {a + nb} > {self.n}"
        self.off = a + nb
        v = self.t[:, a:a + nb]
        if dt != U8:
            v = v.bitcast(dt)
        if len(shape) > 2:
            names = " ".join(f"d{i}" for i in range(1, len(shape)))
            kw = {f"d{i}": int(shape[i]) for i in range(2, len(shape))}
            v = v.rearrange(f"p ({names}) -> p {names}", **kw)
        if shape[0] != 128:
            v = v[0:shape[0]]
        return v
NT_OWN = 4096
NT_ALL = 8192
NCTX = 256
NKA = NT_ALL + NCTX
NKB = NT_OWN + 256 + NCTX
LAMBDA_INIT = 0.8 - 0.6 * 1.0
QSCALE_A = 64 ** -0.5
QSCALE_B = 128 ** -0.5
NEG = -30000.0


def build_nc(debug=False, phases="ABCDE"):
    nc = bass.Bass("TRN2", target_bir_lowering=False)

    def din(name, shape, dt=F32):
        return nc.dram_tensor(name, list(shape), dt, kind="ExternalInput").ap()

    def dscr(name, shape, dt=BF16):
        return nc.dram_tensor(name, list(shape), dt, kind=("ExternalOutput" if debug else "Internal")).ap()

    x_d = din("x_loc", [NT_ALL, 2048])
    ctx_d = din("ctx", [NCTX, 2048])
    cvec_d = din("cvec", [128, 16, 2])
    wmod_d = din("w_mod", [2048, 6144])
    bmod_d = din("b_mod2", [128, 48])
    ng_d = din("norm_g2", [128, 16])
    win_d = din("w_in", [2048, 8192])
    wout_d = din("w_out", [2048, 2048])
    lamv_d = din("lamv", [128, 4, 64])
    subg_d = din("sublng", [128, 1024])
    fgb_d = din("fgb", [128, 2048])
    ropeR_d = din("ropeR", [128, 2, 128])
    ropeC_d = din("ropeC", [128, 2, 64])
    perm_d = din("permm", [128, 128])
    ident_d = din("ident", [128, 128])
    nab_d = din("nab", [8, 128, 3, 640])
    nam_d = din("nam", [128, 3, 640])
    out_d = nc.dram_tensor("out_loc", [NT_OWN, 2048], F32, kind="ExternalOutput").ap()

    qaT_s = dscr("qaT_s", [8, 128, NT_OWN])
    kaT_s = dscr("kaT_s", [8, 128, NKA])
    vA_s = dscr("vA_s", [NKA, 1024])
    gA_s = dscr("gA_s", [NT_OWN, 1024])
    qbT_s = dscr("qbT_s", [8, 128, NT_OWN])
    kbT_s = dscr("kbT_s", [8, 128, NKB])
    vB_s = dscr("vB_s", [NKB, 1024])
    gB_s = dscr("gB_s", [NT_OWN, 1024])
    mixed_s = dscr("mixed_s", [NT_OWN, 2048])
    gate_s = dscr("gate_s", [2048], F32)
    winb_s = nc.dram_tensor("winb_s", [16, 128, 16, 512], BF16, kind="Internal").ap()
    woutb_s = nc.dram_tensor("woutb_s", [4, 128, 16, 512], BF16, kind="Internal").ap()

    es = ExitStack()
    ARENA = 200 * 1024
    art = es.enter_context(nc.sbuf_tensor("arena", [128, ARENA], U8))
    A = Arena(art, ARENA)
    ps_t = es.enter_context(nc.psum_tensor("psall", [128, 8, 512], F32))
    ps_bf_t = ps_t.bitcast(BF16)
    ps = ps_t
    psb = ps_bf_t

    Pg = Prog(nc)
    add = Pg.add
    import os as _os0
    if _os0.environ.get("K_MAXOPS"):
        Pg.limit = int(_os0.environ["K_MAXOPS"])

    ident = A.alloc([128, 128], BF16)
    perm = A.alloc([128, 128], BF16)
    modv = A.alloc([128, 48, 2], F32)
    a_x = A.alloc([128, 16], F32)
    a_c = A.alloc([128, 16], F32)
    nlam = A.alloc([128, 1], F32)
    base_persist = A.off

    add("pool", lambda e: e.dma_start(out=ident, in_=ident_d), writes=["ident"], dma=True)
    add("pool", lambda e: e.dma_start(out=perm, in_=perm_d), writes=["perm"], dma=True)
    def conv_w(cst):
        n = 0
        for (srcw, dstw, nblk, key) in ((win_d, winb_s, 16, "winb"), (wout_d, woutb_s, 4, "woutb")):
            for blk in range(nblk):
                sl = n % 2
                n += 1
                src = srcw[:, blk * 512:(blk + 1) * 512].rearrange("(kc p) n -> p kc n", p=128)
                add("pool", lambda e, sl=sl, src=src: e.dma_start(out=cst[sl], in_=src), writes=[("cst", sl)], dma=True)
                add("sp", lambda e, sl=sl, blk=blk, dstw=dstw: e.dma_start(out=dstw[blk], in_=cst[sl]), reads=[("cst", sl)], writes=[(key, blk)], dma=True)

    if "A" in phases:
        cv = A.alloc([128, 16, 2], F32)
        sc = A.alloc([128, 16, 2], BF16)
        bm = A.alloc([128, 48], F32)
        ng = A.alloc([128, 16], F32)
        lamv = A.alloc([128, 4, 64], F32)
        lprod = A.alloc([128, 2, 64], F32)
        lsum = A.alloc([128, 2], F32)
        wm = [A.alloc([128, 6144], BF16) for _ in range(3)]
        add("sp", lambda e: e.dma_start(out=cv, in_=cvec_d), writes=["cv"], dma=True)
        add("sp", lambda e: e.dma_start(out=bm, in_=bmod_d), writes=["bm"], dma=True)
        add("sp", lambda e: e.dma_start(out=ng, in_=ng_d), writes=["ng"], dma=True)
        add("sp", lambda e: e.dma_start(out=lamv, in_=lamv_d), writes=["lamv"], dma=True)
        add("act", lambda e: e.activation(out=sc, in_=cv, func=AF.Silu), reads=["cv"], writes=["sc"])
        for kc in range(16):
            s = kc % 3
            add("pool", lambda e, s=s, kc=kc: e.dma_start(out=wm[s], in_=wmod_d[kc * 128:(kc + 1) * 128, :]),
                writes=[("wm", s)], dma=True)

            def mm(e, s=s, kc=kc):
                ins = None
                for j in range(48):
                    ins = e.matmul(ps[:, 7, 2 * j:2 * j + 2], lhsT=wm[s][:, j * 128:(j + 1) * 128], rhs=sc[:, kc, :],
                                   start=(kc == 0 and j == 0), stop=(kc == 15), skip_group_check=True)
                return ins
            add("pe", mm, reads=[("wm", s), "sc"], writes=["psM"], banks=[7])
        psM = ps[:, 7, 0:96].rearrange("p (j c) -> p j c", c=2)
        add("dve", lambda e: e.tensor_tensor(out=modv[:, :, 0], in0=psM[:, :, 0], in1=bm, op=ALU.add), reads=["psM", "bm"], writes=["modv0"], banks=[7])
        add("dve", lambda e: e.tensor_tensor(out=modv[:, :, 1], in0=psM[:, :, 1], in1=bm, op=ALU.add), reads=["psM", "bm"], writes=["modv1"], banks=[7])
        add("dve", lambda e: e.scalar_tensor_tensor(out=a_x, in0=modv[:, 16:32, 0], scalar=1.0, in1=ng, op0=ALU.add, op1=ALU.mult),
            reads=["modv0", "ng"], writes=["a_x"])
        add("dve", lambda e: e.scalar_tensor_tensor(out=a_c, in0=modv[:, 16:32, 1], scalar=1.0, in1=ng, op0=ALU.add, op1=ALU.mult),
            reads=["modv1", "ng"], writes=["a_c"])
        import os as _os
        if "gates" not in _os.environ.get("K_SKIP", ""):
            add("sp", lambda e: e.dma_start(out=gate_s.rearrange("(j p) -> p j", p=128), in_=modv[:, 32:48, 0],
                                            allow_slow_non_contiguous=True), reads=["modv0"], writes=["gate_s"], dma=True)
        add("dve", lambda e: e.tensor_tensor(out=lprod[:, 0, :], in0=lamv[:, 0, :], in1=lamv[:, 1, :], op=ALU.mult), reads=["lamv"], writes=["lp0"])
        add("dve", lambda e: e.tensor_tensor(out=lprod[:, 1, :], in0=lamv[:, 2, :], in1=lamv[:, 3, :], op=ALU.mult), reads=["lamv"], writes=["lp1"])
        add("dve", lambda e: e.reduce_sum(out=lsum, in_=lprod, axis=AX.X), reads=["lp0", "lp1"], writes=["lsum"])
        add("act", lambda e: e.activation(out=lsum, in_=lsum, func=AF.Exp), reads=["lsum"], writes=["lsum"])
        add("dve", lambda e: e.scalar_tensor_tensor(out=nlam, in0=lsum[:, 1:2], scalar=-LAMBDA_INIT, in1=lsum[:, 0:1], op0=ALU.add, op1=ALU.subtract),
            reads=["lsum"], writes=["nlam"])
        Pg.barrier()
    b_x = modv[:, 0:16, 0]
    b_c = modv[:, 0:16, 1]

    if "B" in phases:
        A.reset(base_persist)
        xt = [A.alloc([128, 2048], F32) for _ in range(2)]
        xs = [A.alloc([128, 2048], BF16) for _ in range(2)]
        junk = A.alloc([128, 2048], BF16)
        ssq = [A.alloc([128, 1], F32) for _ in range(2)]
        hxT = [A.alloc([128, 16, 512], BF16) for _ in range(2)]
        NW = 4
        wt = [A.alloc([128, 16, 512], BF16) for _ in range(NW)]
        ropeR = A.alloc([128, 2, 128], F32)
        ropeC = A.alloc([128, 2, 64], F32)
        cosT = A.alloc([128, 8, 64], F32)
        sinT = A.alloc([128, 8, 64], F32)
        subg = A.alloc([128, 1024], F32)
        NST = 6
        stg = [A.alloc([128, 512], BF16) for _ in range(NST)]
        q16 = [A.alloc([128, 512], BF16) for _ in range(2)]
        t1 = [A.alloc([128, 512], F32) for _ in range(2)]
        t2 = [A.alloc([128, 512], F32) for _ in range(2)]
        gt = [A.alloc([128, 512], F32) for _ in range(2)]
        add("sp", lambda e: e.dma_start(out=ropeR, in_=ropeR_d), writes=["ropeR"], dma=True)
        add("sp", lambda e: e.dma_start(out=ropeC, in_=ropeC_d), writes=["ropeC"], dma=True)
        add("sp", lambda e: e.dma_start(out=subg, in_=subg_d), writes=["subg"], dma=True)
        add("dve", lambda e: e.tensor_scalar(out=subg, in0=subg, scalar1=1.0 - LAMBDA_INIT, scalar2=None, op0=ALU.mult), reads=["subg"], writes=["subg"])

        st = {"x": 0, "stg": 0, "acc": 0, "r": 0, "w": 0, "k": 0, "ev": 0}
        cosT_f = cosT.rearrange("p a b -> p (a b)")
        sinT_f = sinT.rearrange("p a b -> p (a b)")

        def norm_sub(src, s, hx, hkey, av, bv):
            sl = st["x"] % 2
            st["x"] += 1
            add("sp", lambda e: e.dma_start(out=xt[sl], in_=src[s * 128:(s + 1) * 128, :]), writes=[("xt", sl)], dma=True)
            add("act", lambda e: e.activation(out=junk, in_=xt[sl], func=AF.Square, accum_out=ssq[sl]), reads=[("xt", sl)], writes=[("ssq", sl)])
            add("dve", lambda e: e.tensor_scalar(out=ssq[sl], in0=ssq[sl], scalar1=1.0 / 2048, scalar2=1e-6, op0=ALU.mult, op1=ALU.add),
                reads=[("ssq", sl)], writes=[("ssq", sl)])
            add("act", lambda e: e.activation(out=ssq[sl], in_=ssq[sl], func=AF.Ln), reads=[("ssq", sl)], writes=[("ssq", sl)])
            add("act", lambda e: e.activation(out=ssq[sl], in_=ssq[sl], func=AF.Exp, scale=-0.5), reads=[("ssq", sl)], writes=[("ssq", sl)])
            add("dve", lambda e: e.tensor_scalar(out=xs[sl], in0=xt[sl], scalar1=ssq[sl][:, 0:1], scalar2=None, op0=ALU.mult),
                reads=[("xt", sl), ("ssq", sl)], writes=[("xs", sl)])

            def tr(e):
                ins = None
                for kc in range(16):
                    ins = e.transpose(out=psb[:, kc // 8, (kc % 8) * 128:(kc % 8 + 1) * 128], in_=xs[sl][:, kc * 128:(kc + 1) * 128], identity=ident)
                return ins
            add("pe", tr, reads=[("xs", sl), "ident"], writes=["psX"], banks=[0, 1])
            for kc in range(16):
                src_ps = psb[:, kc // 8, (kc % 8) * 128:(kc % 8 + 1) * 128]
                dst = hxT[hx][:, kc, s * 128:(s + 1) * 128]
                if kc < 8:
                    add("act", lambda e, src_ps=src_ps, dst=dst, kc=kc: e.activation(out=dst, in_=src_ps, func=AF.Identity, scale=av[:, kc:kc + 1], bias=bv[:, kc:kc + 1]),
                        reads=["psX", "a_x", "a_c", "modv0", "modv1"], writes=[(hkey, s, kc)], banks=[kc // 8])
                else:
                    add("dve", lambda e, src_ps=src_ps, dst=dst, kc=kc: e.tensor_scalar(out=dst, in0=src_ps, scalar1=av[:, kc:kc + 1], scalar2=bv[:, kc:kc + 1], op0=ALU.mult, op1=ALU.add),
                        reads=["psX", "a_x", "a_c", "modv0", "modv1"], writes=[(hkey, s, kc)], banks=[kc // 8])

        def hx_reads(hkey, subs):
            return [(hkey, s, kc) for s in subs for kc in range(16)]

        wb_pending = []

        def load_w(c0, first):
            sl = st["w"] % NW
            st["w"] += 1
            blk = c0 // 512
            if first:
                src = win_d[:, c0:c0 + 512].rearrange("(kc p) n -> p kc n", p=128)
                add("pool", lambda e: e.dma_start(out=wt[sl], in_=src), writes=[("wt", sl)], dma=True)
                while wb_pending:
                    wb_pending.pop(0)()
                wb_pending.append(lambda: add("pool", lambda e: e.dma_start(out=winb_s[blk], in_=wt[sl]), reads=[("wt", sl)], writes=[("winb", blk)], dma=True))
            else:
                while wb_pending:
                    wb_pending.pop(0)()
                add("sp", lambda e: e.dma_start(out=wt[sl], in_=winb_s[blk]), reads=[("winb", blk)], writes=[("wt", sl)], dma=True)
            return sl

        def next_acc():
            b = 2 + st["acc"] % 3
            st["acc"] += 1
            return b

        def next_stg():
            k = st["stg"] % NST
            st["stg"] += 1
            return k

        pending = []

        def flush_pending():
            while pending:
                pending.pop(0)()

        def store(k, dst, T=512, rows=128):
            add("pool", lambda e: e.dma_start(out=dst, in_=stg[k][0:rows, 0:T]), reads=[("stg", k)], writes=[("scr", st["k"])], dma=True)
            st["k"] += 1

        def fm_block(wsl, hb, hx, hkey, T, dst, rope):
            while len(pending) > 2:
                pending.pop(0)()
            bank = next_acc()
            nsub = T // 128

            def mm(e):
                ins = None
                for kc in range(16):
                    ins = e.matmul(ps[:, bank, 0:T], lhsT=wt[wsl][:, kc, hb * 128:(hb + 1) * 128], rhs=hxT[hx][:, kc, 0:T], start=(kc == 0), stop=(kc == 15))
                return ins
            add("pe", mm, reads=[("wt", wsl)] + hx_reads(hkey, range(nsub)), writes=[("acc", bank)], banks=[bank])
            if not rope:
                def post():
                    k = next_stg()
                    add("act", lambda e: e.copy(out=stg[k][:, 0:T], in_=ps[:, bank, 0:T]), reads=[("acc", bank)], writes=[("stg", k)], banks=[bank])
                    store(k, dst, T)
                pending.append(post)
            else:
                r = st["r"] % 2
                st["r"] += 1
                rb = 5 + r

                def post():
                    k = next_stg()
                    add("act", lambda e: e.copy(out=q16[r][:, 0:T], in_=ps[:, bank, 0:T]), reads=[("acc", bank)], writes=[("q16", r)], banks=[bank])
                    add("pe", lambda e: e.matmul(ps[:, rb, 0:T], lhsT=perm, rhs=q16[r][:, 0:T], start=True, stop=True),
                        reads=[("q16", r), "perm"], writes=[("psR", rb)], banks=[rb])
                    add("dve", lambda e: e.tensor_tensor(out=t1[r][:, 0:T], in0=ps[:, bank, 0:T], in1=cosT_f[:, 0:T], op=ALU.mult),
                        reads=[("acc", bank), "cosT"], writes=[("t1", r)], banks=[bank])
                    add("dve", lambda e: e.tensor_tensor(out=t2[r][:, 0:T], in0=ps[:, rb, 0:T], in1=sinT_f[:, 0:T], op=ALU.mult),
                        reads=[("psR", rb), "sinT"], writes=[("t2", r)], banks=[rb])
                    add("pool", lambda e: e.tensor_tensor(out=stg[k][:, 0:T], in0=t1[r][:, 0:T], in1=t2[r][:, 0:T], op=ALU.add),
                        reads=[("t1", r), ("t2", r)], writes=[("stg", k)])
                    store(k, dst, T)
                pending.append(post)

        def tm_block(wsl, s, hx, hkey, dst, kind, c0g):
            while len(pending) > 2:
                pending.pop(0)()
            bank = next_acc()

            def mm(e):
                ins = None
                for kc in range(16):
                    ins = e.matmul(ps[:, bank, :], lhsT=hxT[hx][:, kc, s * 128:(s + 1) * 128], rhs=wt[wsl][:, kc, :], start=(kc == 0), stop=(kc == 15))
                return ins
            add("pe", mm, reads=[("wt", wsl)] + hx_reads(hkey, [s]), writes=[("acc", bank)], banks=[bank])

            def post():
                k = next_stg()
                if kind == "v":
                    add("dve", lambda e: e.tensor_copy(out=stg[k], in_=ps[:, bank, :]), reads=[("acc", bank)], writes=[("stg", k)], banks=[bank])
                elif kind == "gb":
                    add("act", lambda e: e.activation(out=stg[k], in_=ps[:, bank, :], func=AF.Silu), reads=[("acc", bank)], writes=[("stg", k)], banks=[bank])
                else:
                    g = st["ev"] % 2
                    st["ev"] += 1
                    add("act", lambda e: e.activation(out=gt[g], in_=ps[:, bank, :], func=AF.Silu), reads=[("acc", bank)], writes=[("gt", g)], banks=[bank])
                    add("pool", lambda e: e.tensor_tensor(out=stg[k], in0=gt[g], in1=subg[:, c0g:c0g + 512], op=ALU.mult),
                        reads=[("gt", g), "subg"], writes=[("stg", k)])
                store(k, dst)
            pending.append(post)

        def rope_tables(t):
            for r in range(8):
                rho = 8 * t + r
                add("dve", lambda e, r=r, rho=rho: e.tensor_scalar(out=cosT[:, r, :], in0=ropeC[:, 0, :], scalar1=ropeR[:, 0, rho:rho + 1], scalar2=None, op0=ALU.add),
                    reads=["ropeR", "ropeC"], writes=["cosT"])
                add("dve", lambda e, r=r, rho=rho: e.tensor_scalar(out=sinT[:, r, :], in0=ropeC[:, 1, :], scalar1=ropeR[:, 1, rho:rho + 1], scalar2=None, op0=ALU.add),
                    reads=["ropeR", "ropeC"], writes=["sinT"])

        tiles = []
        for t in range(16):
            tiles.append(("lat", t))
        tiles.append(("ctx", 0))
        import os as _os
        _kt = _os.environ.get("K_TILES")
        if _kt:
            tiles = [tiles[int(i)] for i in _kt.split(",")]
        _kb = _os.environ.get("K_BLOCKS")

        def tile_src(tl):
            if tl[0] == "lat":
                return x_d[tl[1] * 512:(tl[1] + 1) * 512, :], 4, a_x, b_x
            return ctx_d, 2, a_c, b_c

        def tile_blocks(tl):
            kind, t = tl
            if _kb:
                return [(int(c), None) for c in _kb.split(",")]
            if kind == "lat" and t < 8:
                return [(c, None) for c in range(0, 8192, 512)]
            blocks = [(1024, None), (1536, None), (2048, None), (2560, None)]
            if kind == "ctx" or t == 8:
                blocks += [(5120, None), (5632, None), (6144, None), (6656, None)]
            return blocks

        work = [(ti, c0) for ti, tl in enumerate(tiles) for (c0, _) in tile_blocks(tl)]
        wslots = {}
        wi = {"n": 0}

        def prefetch_w(upto):
            while wi["n"] < min(upto, len(work)):
                wslots[wi["n"]] = load_w(work[wi["n"]][1], work[wi["n"]][0] == 0)
                wi["n"] += 1
        widx = {"n": 0}
        src0, ns0, av0, bv0 = tile_src(tiles[0])
        for s in range(ns0):
            norm_sub(src0, s, 0, ("hx", 0), av0, bv0)
        for ti, tl in enumerate(tiles):
            hx = ti % 2
            hkey = ("hx", hx)
            kind, t = tl
            T = 512 if kind == "lat" else 256
            nsub = T // 128
            tok0 = t * 512 if kind == "lat" else NT_ALL
            if kind == "lat":
                rope_tables(t)
            blocks = tile_blocks(tl)
            nxt = tiles[ti + 1] if ti + 1 < len(tiles) else None
            nxt_subs = []
            if nxt is not None:
                srcn, nsn, avn, bvn = tile_src(nxt)
                nxt_subs = list(range(nsn))
            step = max(1, len(blocks) // (len(nxt_subs) + 1)) if nxt_subs else 0
            for bi, (c0, _) in enumerate(blocks):
                prefetch_w(widx["n"] + 3)
                wsl = wslots[widx["n"]]
                widx["n"] += 1
                grp = c0 // 1024
                hb0 = (c0 % 1024) // 128
                is_own = (kind == "lat" and t < 8)
                if grp in (0, 1, 4, 5):
                    for hb in range(4):
                        h = hb0 + hb
                        if grp == 0:
                            fm_block(wsl, hb, hx, hkey, T, qaT_s[h][:, tok0:tok0 + T], True)
                        elif grp == 1:
                            fm_block(wsl, hb, hx, hkey, T, kaT_s[h][:, tok0:tok0 + T], kind == "lat")
                        elif grp == 4:
                            fm_block(wsl, hb, hx, hkey, T, qbT_s[h][:, tok0:tok0 + T], False)
                        else:
                            if is_own:
                                fm_block(wsl, hb, hx, hkey, T, kbT_s[h][:, tok0:tok0 + T], False)
                            elif kind == "lat":
                                fm_block(wsl, hb, hx, hkey, 256, kbT_s[h][:, NT_OWN:NT_OWN + 256], False)
                            else:
                                fm_block(wsl, hb, hx, hkey, 256, kbT_s[h][:, NT_OWN + 256:NT_OWN + 512], False)
                else:
                    cc = c0 % 1024
                    for s in range(nsub):
                        if grp == 2:
                            tm_block(wsl, s, hx, hkey, vA_s[tok0 + s * 128:tok0 + (s + 1) * 128, cc:cc + 512], "v", cc)
                        elif grp == 3:
                            tm_block(wsl, s, hx, hkey, gA_s[tok0 + s * 128:tok0 + (s + 1) * 128, cc:cc + 512], "ga", cc)
                        elif grp == 7:
                            tm_block(wsl, s, hx, hkey, gB_s[tok0 + s * 128:tok0 + (s + 1) * 128, cc:cc + 512], "gb", cc)
                        else:
                            if is_own:
                                tm_block(wsl, s, hx, hkey, vB_s[tok0 + s * 128:tok0 + (s + 1) * 128, cc:cc + 512], "v", cc)
                            elif kind == "lat":
                                if s < 2:
                                    tm_block(wsl, s, hx, hkey, vB_s[NT_OWN + s * 128:NT_OWN + (s + 1) * 128, cc:cc + 512], "v", cc)
                            else:
                                tm_block(wsl, s, hx, hkey, vB_s[NT_OWN + 256 + s * 128:NT_OWN + 256 + (s + 1) * 128, cc:cc + 512], "v", cc)
                if nxt_subs and (bi + 1) % step == 0:
                    s = nxt_subs.pop(0)
                    norm_sub(srcn, s, 1 - hx, ("hx", 1 - hx), avn, bvn)
            flush_pending()
            while nxt_subs:
                s = nxt_subs.pop(0)
                norm_sub(srcn, s, 1 - hx, ("hx", 1 - hx), avn, bvn)
        Pg.barrier()

    if "C" in phases:
        A.reset(base_persist)
        kaT = [A.alloc([128, NKA], BF16) for _ in range(2)]
        vaA = [A.alloc([128, 66, 129], BF16) for _ in range(2)]
        qt = [A.alloc([128, 512], BF16) for _ in range(2)]
        Gt = [A.alloc([128, 4, 128], BF16) for _ in range(2)]
        NP = 3
        Pt = [A.alloc([128, 2, 512], BF16) for _ in range(NP)]
        rl = A.alloc([128, 4, 2], F32)
        nl2 = A.alloc([128, 4], F32)
        t1c = A.alloc([128, 128], F32)
        Dd = A.alloc([128, 4, 128], F32)
        sqj = A.alloc([128, 128], F32)
        ssq4 = A.alloc([128, 4], F32)
        mst = [A.alloc([128, 4, 128], BF16) for _ in range(2)]
        for i in range(2):
            add("pool", lambda e, i=i: e.memset(vaA[i][:, :, 128:129], 1.0), writes=[("va1", i)])
        psS = [ps[:, 0:2, :], ps[:, 2:4, :]]

        def ogrp(g):
            return ps[:, 4 + g // 3, (g % 3) * 129:(g % 3) * 129 + 129]
        cnt = {"s": 0, "p": 0, "q": 0}
        for h in range(8):
            hs = h % 2
            add("sp", lambda e, h=h, hs=hs: e.dma_start(out=kaT[hs], in_=kaT_s[h]), writes=[("kaT", hs)], dma=True)
            add("sp", lambda e, h=h, hs=hs: e.dma_start(out=vaA[hs][:, :, 0:128], in_=vA_s[:, h * 128:(h + 1) * 128].rearrange("(c p) d -> p c d", p=128)),
                writes=[("vaA", hs)], dma=True)
            for qb in range(8):
                qs = cnt["q"] % 2
                cnt["q"] += 1
                add("sp", lambda e, h=h, qb=qb, qs=qs: e.dma_start(out=qt[qs], in_=qaT_s[h][:, qb * 512:(qb + 1) * 512]), writes=[("qt", qs)], dma=True)
                add("sp", lambda e, h=h, qb=qb, qs=qs: e.dma_start(out=Gt[qs], in_=gA_s[qb * 512:(qb + 1) * 512, h * 128:(h + 1) * 128].rearrange("(s p) d -> p s d", p=128)),
                    writes=[("Gt", qs)], dma=True)

                def S_op(kc, hs=hs, qs=qs):
                    sl = cnt["s"] % 2
                    cnt["s"] += 1

                    def f(e):
                        e.matmul(psS[sl][:, 0, :], lhsT=kaT[hs][0:64, kc * 128:(kc + 1) * 128], rhs=qt[qs][0:64, :], start=True, stop=True)
                        return e.matmul(psS[sl][:, 1, :], lhsT=kaT[hs][64:128, kc * 128:(kc + 1) * 128], rhs=qt[qs][64:128, :], start=True, stop=True)
                    add("pe", f, reads=[("kaT", hs), ("qt", qs)], writes=[("S", sl)], banks=[2 * sl, 2 * sl + 1])
                    return sl
                sls = {0: S_op(0), 1: S_op(1)}
                for kc in range(66):
                    sl = sls[kc]
                    pl = cnt["p"] % NP
                    cnt["p"] += 1
                    add("act", lambda e, sl=sl, pl=pl: e.activation(out=Pt[pl], in_=psS[sl], func=AF.Exp, scale=QSCALE_A),
                        reads=[("S", sl)], writes=[("P", pl)], banks=[2 * sl, 2 * sl + 1])
                    if kc + 2 < 66:
                        sls[kc + 2] = S_op(kc + 2)

                    def pv(e, kc=kc, pl=pl, hs=hs):
                        ins = None
                        for m in range(2):
                            for s in range(4):
                                g = m * 4 + s
                                ins = e.matmul(ogrp(g), lhsT=Pt[pl][:, m, s * 128:(s + 1) * 128], rhs=vaA[hs][:, kc, :],
                                               start=(kc == 0 and g % 3 == 0), stop=(kc == 65), skip_group_check=True)
                        return ins
                    add("pe", pv, reads=[("P", pl), ("vaA", hs), ("va1", hs)], writes=["O"], banks=[4, 5, 6])
                ms = (h * 8 + qb) % 2
                for s in range(4):
                    o1 = ogrp(s)
                    o2 = ogrp(4 + s)
                    add("dve", lambda e, s=s, o1=o1: e.reciprocal(out=rl[:, s, 0:1], in_=o1[:, 128:129]), reads=["O"], writes=[("rl", s)], banks=[4, 5, 6])
                    add("dve", lambda e, s=s, o2=o2: e.reciprocal(out=rl[:, s, 1:2], in_=o2[:, 128:129]), reads=["O"], writes=[("rl2", s)], banks=[4, 5, 6])
                    add("dve", lambda e, s=s: e.tensor_scalar(out=nl2[:, s:s + 1], in0=rl[:, s, 1:2], scalar1=nlam[:, 0:1], scalar2=None, op0=ALU.mult),
                        reads=[("rl2", s), "nlam"], writes=[("nl2", s)])
                    add("dve", lambda e, s=s, o1=o1: e.tensor_scalar(out=t1c, in0=o1[:, 0:128], scalar1=rl[:, s, 0:1], scalar2=None, op0=ALU.mult),
                        reads=["O", ("rl", s)], writes=["t1c"], banks=[4, 5, 6])
                    add("dve", lambda e, s=s, o2=o2: e.scalar_tensor_tensor(out=Dd[:, s, :], in0=o2[:, 0:128], scalar=nl2[:, s:s + 1], in1=t1c, op0=ALU.mult, op1=ALU.add),
                        reads=["O", ("nl2", s), "t1c"], writes=[("D", s)], banks=[4, 5, 6])
                    add("dve", lambda e, s=s: e.scalar_tensor_tensor(out=sqj, in0=Dd[:, s, :], scalar=1.0, in1=Dd[:, s, :], op0=ALU.mult, op1=ALU.mult, accum_out=ssq4[:, s:s + 1]),
                        reads=[("D", s)], writes=["ssq4"])
                add("dve", lambda e: e.tensor_scalar(out=ssq4, in0=ssq4, scalar1=1.0 / 128, scalar2=1e-5, op0=ALU.mult, op1=ALU.add),
                    reads=["ssq4"], writes=["ssq4"])
                add("act", lambda e: e.activation(out=ssq4, in_=ssq4, func=AF.Ln), reads=["ssq4"], writes=["ssq4"])
                add("act", lambda e: e.activation(out=ssq4, in_=ssq4, func=AF.Exp, scale=-0.5), reads=["ssq4"], writes=["ssq4"])
                for s in range(4):
                    add("dve", lambda e, s=s, ms=ms, qs=qs: e.scalar_tensor_tensor(out=mst[ms][:, s, :], in0=Dd[:, s, :], scalar=ssq4[:, s:s + 1], in1=Gt[qs][:, s, :], op0=ALU.mult, op1=ALU.mult),
                        reads=[("D", s), "ssq4", ("Gt", qs)], writes=[("mst", ms, s)])
                add("pool", lambda e, h=h, qb=qb, ms=ms: e.dma_start(out=mixed_s[qb * 512:(qb + 1) * 512, h * 128:(h + 1) * 128].rearrange("(s p) d -> p s d", p=128), in_=mst[ms]),
                    reads=[("mst", ms, s) for s in range(4)], writes=[("mixA", h, qb)], dma=True)
        Pg.barrier()

    if "D" in phases:
        A.reset(base_persist)
        kbT = [A.alloc([128, NKB], BF16) for _ in range(2)]
        vbB = [A.alloc([128, 36, 129], BF16) for _ in range(2)]
        nabt = [A.alloc([128, 3, 640], F32) for _ in range(2)]
        namt = A.alloc([128, 3, 640], F32)
        qtb = [A.alloc([128, 512], BF16) for _ in range(2)]
        Gtb = [A.alloc([128, 4, 128], BF16) for _ in range(2)]
        tmpb = [A.alloc([128, 640], F32) for _ in range(2)]
        Ptb = [A.alloc([128, 896], BF16) for _ in range(3)]
        rlb = A.alloc([128, 4], F32)
        t1b = A.alloc([128, 128], F32)
        mstb = [A.alloc([128, 4, 128], BF16) for _ in range(2)]
        for i in range(2):
            add("pool", lambda e, i=i: e.memset(vbB[i][:, :, 128:129], 1.0), writes=[("vb1", i)])
        add("sp", lambda e: e.dma_start(out=namt, in_=nam_d), writes=["namt"], dma=True)
        psSb = [ps[:, 0:2, :].rearrange("p a b -> p (a b)"), ps[:, 2:4, :].rearrange("p a b -> p (a b)")]

        def ogrpb(g):
            return ps[:, 4 + g // 3, (g % 3) * 129:(g % 3) * 129 + 129]
        cn = {"s": 0, "p": 0, "q": 0}
        for h in range(8):
            hs = h % 2
            add("sp", lambda e, h=h, hs=hs: e.dma_start(out=kbT[hs], in_=kbT_s[h]), writes=[("kbT", hs)], dma=True)
            add("sp", lambda e, h=h, hs=hs: e.dma_start(out=vbB[hs][:, :, 0:128], in_=vB_s[:, h * 128:(h + 1) * 128].rearrange("(c p) d -> p c d", p=128)),
                writes=[("vbB", hs)], dma=True)
            add("sp", lambda e, h=h, hs=hs: e.dma_start(out=nabt[hs], in_=nab_d[h]), writes=[("nab", hs)], dma=True)
            add("dve", lambda e, hs=hs: e.tensor_tensor(out=nabt[hs], in0=nabt[hs], in1=namt, op=ALU.add), reads=[("nab", hs), "namt"], writes=[("nab", hs)])
            qslot = {}

            def loads(qg, h=h):
                qs = cn["q"] % 2
                cn["q"] += 1
                qslot[qg] = qs
                add("sp", lambda e: e.dma_start(out=qtb[qs], in_=qbT_s[h][:, qg * 512:(qg + 1) * 512]), writes=[("qtb", qs)], dma=True)
                add("sp", lambda e: e.dma_start(out=Gtb[qs], in_=gB_s[qg * 512:(qg + 1) * 512, h * 128:(h + 1) * 128].rearrange("(s p) d -> p s d", p=128)),
                    writes=[("Gtb", qs)], dma=True)
            info = {}

            def issue_S(i, hs=hs):
                qg, s = i // 4, i % 4
                if qg not in qslot:
                    loads(qg)
                qs = qslot[qg]
                tq = i
                cs = max(tq - 2, 0)
                pat = tq if tq < 2 else 2
                sl = cn["s"] % 2
                cn["s"] += 1
                chunks = [cs + ci for ci in range(5)] + [34, 35]

                def sm(e):
                    ins = None
                    for c, ch in enumerate(chunks):
                        ins = e.matmul(psSb[sl][:, c * 128:(c + 1) * 128], lhsT=kbT[hs][:, ch * 128:(ch + 1) * 128], rhs=qtb[qs][:, s * 128:(s + 1) * 128], start=True, stop=True)
                    return ins
                add("pe", sm, reads=[("kbT", hs), ("qtb", qs)], writes=[("Sb", sl)], banks=[2 * sl, 2 * sl + 1])
                info[i] = (sl, chunks, pat)
            issue_S(0)
            issue_S(1)
            for i in range(32):
                qg, s = i // 4, i % 4
                qs = qslot[qg]
                sl, chunks, pat = info[i]
                ms = (h * 8 + qg) % 2
                tb = cn["p"] % 2
                pl = cn["p"] % 3
                cn["p"] += 1
                add("dve", lambda e, sl=sl, tb=tb, hs=hs, pat=pat: e.scalar_tensor_tensor(out=tmpb[tb], in0=psSb[sl][:, 0:640], scalar=QSCALE_B, in1=nabt[hs][:, pat, :], op0=ALU.mult, op1=ALU.add),
                    reads=[("Sb", sl), ("nab", hs)], writes=[("tmpb", tb)], banks=[2 * sl, 2 * sl + 1])
                add("act", lambda e, sl=sl, pl=pl: e.activation(out=Ptb[pl][:, 640:896], in_=psSb[sl][:, 640:896], func=AF.Exp, scale=QSCALE_B), reads=[("Sb", sl)], writes=[("Pbc", pl)], banks=[2 * sl + 1])
                add("act", lambda e, tb=tb, pl=pl: e.activation(out=Ptb[pl][:, 0:640], in_=tmpb[tb], func=AF.Exp), reads=[("tmpb", tb)], writes=[("Pbw", pl)])
                if i + 2 < 32:
                    issue_S(i + 2)

                def pvb(e, pl=pl, chunks=chunks, hs=hs, s=s):
                    ins = None
                    for c, ch in enumerate(chunks):
                        ins = e.matmul(ogrpb(s), lhsT=Ptb[pl][:, c * 128:(c + 1) * 128], rhs=vbB[hs][:, ch, :], start=(c == 0 and s % 3 == 0), stop=(c == 6), skip_group_check=True)
                    return ins
                add("pe", pvb, reads=[("Pbw", pl), ("Pbc", pl), ("vbB", hs), ("vb1", hs)], writes=[("Ob", s)], banks=[4 + s // 3])
                if s == 3:
                    for s2 in range(4):
                        ob = ogrpb(s2)
                        add("dve", lambda e, s2=s2, ob=ob: e.reciprocal(out=rlb[:, s2:s2 + 1], in_=ob[:, 128:129]), reads=[("Ob", s2)], writes=[("rlb", s2)], banks=[4 + s2 // 3])
                        add("dve", lambda e, s2=s2, ob=ob, ms=ms, qs=qs: e.scalar_tensor_tensor(out=mstb[ms][:, s2, :], in0=ob[:, 0:128], scalar=rlb[:, s2:s2 + 1], in1=Gtb[qs][:, s2, :], op0=ALU.mult, op1=ALU.mult),
                            reads=[("Ob", s2), ("rlb", s2), ("Gtb", qs)], writes=[("mstb", ms, s2)], banks=[4 + s2 // 3])
                    add("pool", lambda e, h=h, qg=qg, ms=ms: e.dma_start(out=mixed_s[qg * 512:(qg + 1) * 512, 1024 + h * 128:1024 + (h + 1) * 128].rearrange("(s p) d -> p s d", p=128), in_=mstb[ms]),
                        reads=[("mstb", ms, s2) for s2 in range(4)], writes=[("mixB", h, qg)], dma=True)
        Pg.barrier()

    if "E" in phases:
        A.reset(base_persist)
        wo = A.alloc([128, 16, 2048], BF16)
        gbc = A.alloc([128, 2048], F32)
        fgs = A.alloc([128, 2048], F32)
        mt = [A.alloc([128, 2048], BF16) for _ in range(2)]
        mT = [A.alloc([128, 16, 128], BF16) for _ in range(2)]
        xe = [A.alloc([128, 2048], F32) for _ in range(2)]
        hsb = [A.alloc([128, 2048], F32) for _ in range(2)]
        ob_ = [A.alloc([128, 2048], F32) for _ in range(2)]
        junk2 = A.alloc([128, 2048], BF16)
        sse = [A.alloc([128, 1], F32) for _ in range(2)]
        for blk in range(4):
            add("pool", lambda e, blk=blk: e.dma_start(out=wo[:, :, blk * 512:(blk + 1) * 512], in_=wout_d[:, blk * 512:(blk + 1) * 512].rearrange("(kc p) n -> p kc n", p=128)), writes=[("wo", blk)], dma=True)
        add("sp", lambda e: e.dma_start(out=gbc, in_=gate_s.partition_broadcast(128)), reads=["gate_s"], writes=["gbc"], dma=True)
        add("sp", lambda e: e.dma_start(out=fgs, in_=fgb_d), writes=["fgs"], dma=True)
        psY = ps[:, 4:8, :].rearrange("p a b -> p (a b)")
        def e_head(ti):
            sl = ti % 2
            add("sp", lambda e, ti=ti, sl=sl: e.dma_start(out=mt[sl], in_=mixed_s[ti * 128:(ti + 1) * 128, :]), writes=[("mt", sl)], dma=True)
            add("sp", lambda e, ti=ti, sl=sl: e.dma_start(out=xe[sl], in_=x_d[ti * 128:(ti + 1) * 128, :]), writes=[("xe", sl)], dma=True)

            def tr(e, sl=sl):
                ins = None
                for kc in range(16):
                    ins = e.transpose(out=psb[:, kc // 8, (kc % 8) * 128:(kc % 8 + 1) * 128], in_=mt[sl][:, kc * 128:(kc + 1) * 128], identity=ident)
                return ins
            add("pe", tr, reads=[("mt", sl), "ident"], writes=["psXe"], banks=[0, 1])
            add("act", lambda e, sl=sl: e.copy(out=mT[sl][:, 0:8, :], in_=psb[:, 0, :].rearrange("p (a b) -> p a b", b=128)), reads=["psXe"], writes=[("mT0", sl)], banks=[0])
            add("dve", lambda e, sl=sl: e.tensor_copy(out=mT[sl][:, 8:16, :], in_=psb[:, 1, :].rearrange("p (a b) -> p a b", b=128)), reads=["psXe"], writes=[("mT1", sl)], banks=[1])


        def e_mid(ti):
            sl = ti % 2
            for hf in range(2):
                def mm(e, sl=sl, hf=hf):
                    ins = None
                    for nb in range(2 * hf, 2 * hf + 2):
                        for kc in range(16):
                            ins = e.matmul(ps[:, 4 + nb, :], lhsT=mT[sl][:, kc, :], rhs=wo[:, kc, nb * 512:(nb + 1) * 512], start=(kc == 0), stop=(kc == 15))
                    return ins
                add("pe", mm, reads=[("mT0", sl), ("mT1", sl)] + [("wo", b) for b in range(4)], writes=[("psY", hf)], banks=[4 + 2 * hf, 5 + 2 * hf])
                psYh = ps[:, 4 + 2 * hf:6 + 2 * hf, :].rearrange("p a b -> p (a b)")
                add("dve", lambda e, sl=sl, hf=hf, psYh=psYh: e.tensor_tensor(out=hsb[sl][:, hf * 1024:(hf + 1) * 1024], in0=psYh, in1=gbc[:, hf * 1024:(hf + 1) * 1024], op=ALU.mult),
                    reads=[("psY", hf), "gbc"], writes=[("hs", sl, hf)], banks=[4 + 2 * hf, 5 + 2 * hf])
                add("pool", lambda e, sl=sl, hf=hf: e.tensor_tensor(out=hsb[sl][:, hf * 1024:(hf + 1) * 1024], in0=hsb[sl][:, hf * 1024:(hf + 1) * 1024], in1=xe[sl][:, hf * 1024:(hf + 1) * 1024], op=ALU.add),
                    reads=[("hs", sl, hf), ("xe", sl)], writes=[("hs", sl, hf)])

        def e_tail(ti):
            sl = ti % 2
            add("act", lambda e, sl=sl: e.activation(out=junk2, in_=hsb[sl], func=AF.Square, accum_out=sse[sl]), reads=[("hs", sl, 0), ("hs", sl, 1)], writes=[("sse", sl)])
            add("dve", lambda e, sl=sl: e.tensor_scalar(out=sse[sl], in0=sse[sl], scalar1=1.0 / 2048, scalar2=1e-6, op0=ALU.mult, op1=ALU.add), reads=[("sse", sl)], writes=[("sse", sl)])
            add("act", lambda e, sl=sl: e.activation(out=sse[sl], in_=sse[sl], func=AF.Ln), reads=[("sse", sl)], writes=[("sse", sl)])
            add("act", lambda e, sl=sl: e.activation(out=sse[sl], in_=sse[sl], func=AF.Exp, scale=-0.5), reads=[("sse", sl)], writes=[("sse", sl)])
            add("dve", lambda e, sl=sl: e.scalar_tensor_tensor(out=ob_[sl], in0=hsb[sl], scalar=sse[sl][:, 0:1], in1=fgs, op0=ALU.mult, op1=ALU.mult),
                reads=[("hs", sl, 0), ("hs", sl, 1), ("sse", sl), "fgs"], writes=[("ob", sl)])
            add("pool", lambda e, ti=ti, sl=sl: e.dma_start(out=out_d[ti * 128:(ti + 1) * 128, :], in_=ob_[sl]), reads=[("ob", sl)], writes=[("out", ti)], dma=True)

        e_head(0)
        for ti in range(32):
            e_mid(ti)
            if ti + 1 < 32:
                e_head(ti + 1)
            e_tail(ti)
        Pg.barrier()

    if Pg.limit is not None:
        print("ops recorded", len(Pg.ops), [(o.eng, o.oid) for o in Pg.ops[-3:]])
    Pg.limit = None
    Pg.barrier()
    add("sp", lambda e: e.wait_ge(Pg.sems[("dma", ("sp", 0))], 16))
    Pg.finalize(es)
    with nc.Block() as block:
        Pg.emit(block)
    es.close()
    return nc


def _grow(half, rho):
    return rho if half == 0 else 127 - rho


def _rope_tables(half):
    f32 = np.float32
    inv = (np.float32(10000.0) ** (-np.arange(16, dtype=f32) / np.float32(16))).astype(f32)
    R = np.zeros((128, 2, 128), f32)
    C = np.zeros((128, 2, 64), f32)
    rows = np.array([_grow(half, r) for r in range(128)], f32)
    cols = np.arange(64, dtype=f32)
    for p in range(128):
        d = p % 64
        fr = inv[p % 16]
        sgn = f32(-1.0) if (p % 32) < 16 else f32(1.0)
        if d < 32:
            ang = (rows * fr).astype(f32)
            R[p, 0] = np.cos(ang).astype(f32)
            R[p, 1] = sgn * np.sin(ang).astype(f32)
        else:
            ang = (cols * fr).astype(f32)
            C[p, 0] = np.cos(ang).astype(f32)
            C[p, 1] = sgn * np.sin(ang).astype(f32)
    return R, C


def _perm_matrix():
    Pm = np.zeros((128, 128), np.float32)
    for p in range(128):
        partner = p + 16 if (p % 32) < 16 else p - 16
        Pm[partner, p] = 1.0
    return Pm


def _na_tables(half, rpb):
    nab = np.zeros((8, 128, 3, 640), np.float32)
    nam = np.full((128, 3, 640), NEG, np.float32)
    j = np.arange(64)
    c0 = np.clip(j - 8, 0, 48)
    kc = np.arange(64)
    incol = (kc[:, None] >= c0[None, :]) & (kc[:, None] <= c0[None, :] + 15)
    coff = kc[:, None] - j[None, :] + 15
    coff_c = np.clip(coff, 0, 30)
    for pi, tq in enumerate((0, 1, 2)):
        cs = max(tq - 2, 0)
        for ci in range(5):
            lc = cs + ci
            for a in range(2):
                kr = _grow(half, 2 * lc + a)
                for dr in range(2):
                    r = _grow(half, 2 * tq + dr)
                    r0 = min(max(r - 4, 0), 120)
                    if not (r0 <= kr <= r0 + 7):
                        continue
                    roff = kr - r + 7
                    sl = slice(ci * 128 + dr * 64, ci * 128 + dr * 64 + 64)
                    nam[a * 64:(a + 1) * 64, pi, sl] = np.where(incol, 0.0, NEG)
                    vals = rpb[:, roff, :][:, coff_c]
                    nab[:, a * 64:(a + 1) * 64, pi, sl] = np.where(incol[None], vals, 0.0)
    return nab, nam


def _core_inputs(b, half, inp, shared):
    f32 = np.float32
    x = inp["x"]
    rows = [_grow(half, r) for r in range(128)]
    x_loc = np.ascontiguousarray(x[b].reshape(128, 64, 2048)[rows].reshape(8192, 2048))
    cvec = np.stack([inp["c"][b].reshape(16, 128).T, inp["c_ctx"].reshape(16, 128).T], axis=-1)
    R, C = shared["rope"][half]
    nab, nam = shared["na"][half]
    return {
        "x_loc": x_loc,
        "ctx": np.ascontiguousarray(inp["ctx"][b]),
        "cvec": np.ascontiguousarray(cvec.astype(f32)),
        "w_mod": shared["w_mod"], "b_mod2": shared["b_mod2"], "norm_g2": shared["norm_g2"],
        "w_in": shared["w_in"], "w_out": shared["w_out"], "lamv": shared["lamv"],
        "sublng": shared["sublng"], "fgb": shared["fgb"],
        "ropeR": R, "ropeC": C, "permm": shared["perm"], "ident": shared["ident"],
        "nab": nab, "nam": nam,
    }


def _shared(inp):
    f32 = np.float32
    sh = {}
    sh["w_mod"] = np.ascontiguousarray(inp["w_mod"][0], dtype=f32)
    sh["b_mod2"] = np.ascontiguousarray(inp["b_mod"][0].reshape(48, 128).T, dtype=f32)
    sh["norm_g2"] = np.ascontiguousarray(inp["norm_g"][0].reshape(16, 128).T, dtype=f32)
    sh["w_in"] = np.ascontiguousarray(inp["w_in"][0], dtype=f32)
    sh["w_out"] = np.ascontiguousarray(inp["w_out"][0], dtype=f32)
    lam = np.stack([inp["lam_q1"][0], inp["lam_k1"][0], inp["lam_q2"][0], inp["lam_k2"][0]], 0)
    sh["lamv"] = np.ascontiguousarray(np.broadcast_to(lam[None], (128, 4, 64)), dtype=f32)
    sh["sublng"] = np.ascontiguousarray(np.broadcast_to(np.tile(inp["subln_g"][0], 8)[None], (128, 1024)), dtype=f32)
    sh["fgb"] = np.ascontiguousarray(np.broadcast_to(inp["final_g"][None], (128, 2048)), dtype=f32)
    sh["perm"] = _perm_matrix()
    sh["ident"] = np.eye(128, dtype=f32)
    sh["rope"] = [_rope_tables(h) for h in range(2)]
    rpb = np.asarray(inp["rpb"][0], dtype=f32)
    sh["na"] = [_na_tables(h, rpb) for h in range(2)]
    return sh


_NC_CACHE = {}


def kernel(**inputs):
    inp = {k: np.asarray(v) for k, v in inputs.items()}
    sh = _shared(inp)
    in_maps = []
    for core in range(8):
        b, half = core // 2, core % 2
        in_maps.append(_core_inputs(b, half, inp, sh))
    if "nc" not in _NC_CACHE:
        _NC_CACHE["nc"] = build_nc()
    nc = _NC_CACHE["nc"]
    res = run_bass_kernel_spmd(nc, in_maps, core_ids=list(range(8)))
    out = np.empty((4, 8192, 2048), np.float32)
    for core in range(8):
        b, half = core // 2, core % 2
        o = np.asarray(res.results[core]["out_loc"]).reshape(64, 64, 2048)
        rows = [_grow(half, r) for r in range(64)]
        out[b].reshape(128, 64, 2048)[rows] = o
    return out
```

```python
import numpy as np
from concourse.bass_utils import run_bass_kernel_spmd
from contextlib import ExitStack
import concourse.bass as bass
import concourse.mybir as mybir

F32 = mybir.dt.float32
BF16 = mybir.dt.bfloat16
U8 = mybir.dt.uint8
AF = mybir.ActivationFunctionType
ALU = mybir.AluOpType
AX = mybir.AxisListType


class _Op:
    __slots__ = ("eng", "fn", "deps", "dma", "signal", "ev", "oid")

    def __init__(self, eng, fn, deps, dma, oid):
        self.eng = eng
        self.fn = fn
        self.deps = deps
        self.dma = dma
        self.signal = False
        self.ev = None
        self.oid = oid


class Prog:
    COMPUTE = ("pe", "act", "dve", "pool")
    EPOCH = 30000
    NDMA = 8

    def __init__(self, nc, same_engine_sync=True):
        self.nc = nc
        self.ops = []
        self.res_w = {}
        self.res_r = {}
        self.same = same_engine_sync
        self.dma_cnt = {}
        self.dma_last = {}
        self.barrier_deps = set()
        self.last_op = {}
        self.all_dma = []
        self.bank_last = {}

    limit = None

    def add(self, eng, fn, reads=(), writes=(), dma=False, banks=()):
        oid = len(self.ops)
        if self.limit is not None and oid >= self.limit:
            return None
        deps = set(self.barrier_deps)
        for b in banks:
            last = self.bank_last.get(b)
            if last is not None and self.ops[last].eng != eng:
                deps.add(last)
            self.bank_last[b] = oid
        for r in reads:
            w = self.res_w.get(r)
            if w is not None:
                deps.add(w)
        for wr in writes:
            w = self.res_w.get(wr)
            if w is not None:
                deps.add(w)
            for rd in self.res_r.get(wr, ()):
                deps.add(rd)
        if dma:
            i = self.dma_cnt.get(eng, 0)
            self.dma_cnt[eng] = i + 1
            key = (eng, i % self.NDMA)
            prev = self.dma_last.get(key)
            if prev is not None:
                deps.add(prev)
            self.dma_last[key] = oid
        op = _Op(eng, fn, deps, dma, oid)
        if self.limit is not None:
            print("OP", oid, eng, "dma" if dma else "", list(writes)[:2])
        if dma:
            op.ev = ("dma", key, 16 * (i // self.NDMA + 1))
            op.signal = True
            self.all_dma.append(oid)
        self.ops.append(op)
        for r in reads:
            self.res_r.setdefault(r, []).append(oid)
        for wr in writes:
            self.res_w[wr] = oid
            self.res_r[wr] = []
        self.last_op[eng] = oid
        return oid

    def barrier(self):
        deps = set(self.dma_last.values())
        for e, o in self.last_op.items():
            if not self.ops[o].dma:
                deps.add(o)
        self.barrier_deps = deps

    def _needs_sync(self, op, dep):
        if dep.dma:
            return True
        if dep.eng != op.eng:
            return True
        if op.dma:
            return True
        if op.eng == "pe":
            return False
        return self.same

    def finalize(self, es):
        nc = self.nc
        ops = self.ops
        for op in ops:
            for d in op.deps:
                dep = ops[d]
                if not dep.dma and self._needs_sync(op, dep):
                    dep.signal = True
        cnt = {e: 0 for e in self.COMPUTE}
        for op in ops:
            if op.dma or not op.signal:
                continue
            c = cnt[op.eng]
            op.ev = ("eng", op.eng, c // self.EPOCH, c % self.EPOCH + 1)
            cnt[op.eng] = c + 1
        self.sems = {}
        for e in self.COMPUTE:
            n_ep = (cnt[e] + self.EPOCH - 1) // self.EPOCH
            for k in range(max(n_ep, 1)):
                self.sems[("eng", e, k)] = es.enter_context(nc.semaphore(f"s_{e}_{k}"))
        for q, n in self.dma_cnt.items():
            for k in range(min(n, self.NDMA)):
                self.sems[("dma", (q, k))] = es.enter_context(nc.semaphore(f"d_{q}_{k}"))
        self.n_signal = cnt

    def emit_engine(self, eng, e):
        ops = self.ops
        seen_eng = {}
        seen_dma = {}
        for op in ops:
            if op.eng != eng:
                continue
            need_eng = {}
            need_dma = {}
            for d in op.deps:
                dep = ops[d]
                if not self._needs_sync(op, dep):
                    continue
                ev = dep.ev
                if ev[0] == "dma":
                    if need_dma.get(ev[1], 0) < ev[2]:
                        need_dma[ev[1]] = ev[2]
                else:
                    cur = need_eng.get(ev[1])
                    if cur is None or cur < (ev[2], ev[3]):
                        need_eng[ev[1]] = (ev[2], ev[3])
            for k, v in need_dma.items():
                if seen_dma.get(k, 0) < v:
                    e.wait_ge(self.sems[("dma", k)], v)
                    seen_dma[k] = v
            for k, v in need_eng.items():
                cur = seen_eng.get(k)
                if cur is None or cur < v:
                    e.wait_ge(self.sems[("eng", k, v[0])], v[1])
                    seen_eng[k] = v
            ins = op.fn(e)
            if op.signal:
                if op.dma:
                    ins.then_inc(self.sems[("dma", op.ev[1])], 16)
                else:
                    ins.then_inc(self.sems[("eng", op.eng, op.ev[2])], 1)

    def emit(self, block):
        P = self

        @block.sync
        def _(e):
            P.emit_engine("sp", e)

        @block.tensor
        def _(e):
            P.emit_engine("pe", e)

        @block.scalar
        def _(e):
            P.emit_engine("act", e)

        @block.vector
        def _(e):
            P.emit_engine("dve", e)

        @block.gpsimd
        def _(e):
            P.emit_engine("pool", e)


class Arena:
    def __init__(self, tensor, nbytes):
        self.t = tensor
        self.n = nbytes
        self.off = 0

    def reset(self, off=0):
        self.off = off

    def alloc(self, shape, dt):
        esz = mybir.dt.size(dt)
        free = int(np.prod(shape[1:]))
        nb = free * esz
        a = (self.off + 63) // 64 * 64
        assert a + nb <= self.n, f"arena overflow: {a + nb} > {self.n}"
        self.off = a + nb
        v = self.t[:, a:a + nb]
        if dt != U8:
            v = v.bitcast(dt)
        if len(shape) > 2:
            names = " ".join(f"d{i}" for i in range(1, len(shape)))
            kw = {f"d{i}": int(shape[i]) for i in range(2, len(shape))}
            v = v.rearrange(f"p ({names}) -> p {names}", **kw)
        if shape[0] != 128:
            v = v[0:shape[0]]
        return v
NT_OWN = 4096
NT_ALL = 8192
NCTX = 256
NKA = NT_ALL + NCTX
NKB = NT_OWN + 256 + NCTX
LAMBDA_INIT = 0.8 - 0.6 * 1.0
QSCALE_A = 64 ** -0.5
QSCALE_B = 128 ** -0.5
NEG = -30000.0


def build_nc(debug=False, phases="ABCDE"):
    nc = bass.Bass("TRN2", target_bir_lowering=False)

    def din(name, shape, dt=F32):
        return nc.dram_tensor(name, list(shape), dt, kind="ExternalInput").ap()

    def dscr(name, shape, dt=BF16):
        return nc.dram_tensor(name, list(shape), dt, kind=("ExternalOutput" if debug else "Internal")).ap()

    x_d = din("x_loc", [NT_ALL, 2048])
    ctx_d = din("ctx", [NCTX, 2048])
    cvec_d = din("cvec", [128, 16, 2])
    wmod_d = din("w_mod", [2048, 6144])
    bmod_d = din("b_mod2", [128, 48])
    ng_d = din("norm_g2", [128, 16])
    win_d = din("w_in", [2048, 8192])
    wout_d = din("w_out", [2048, 2048])
    lamv_d = din("lamv", [128, 4, 64])
    subg_d = din("sublng", [128, 1024])
    fgb_d = din("fgb", [128, 2048])
    ropeR_d = din("ropeR", [128, 2, 128])
    ropeC_d = din("ropeC", [128, 2, 64])
    perm_d = din("permm", [128, 128])
    ident_d = din("ident", [128, 128])
    nab_d = din("nab", [8, 128, 3, 640])
    nam_d = din("nam", [128, 3, 640])
    out_d = nc.dram_tensor("out_loc", [NT_OWN, 2048], F32, kind="ExternalOutput").ap()

    qaT_s = dscr("qaT_s", [8, 128, NT_OWN])
    kaT_s = dscr("kaT_s", [8, 128, NKA])
    vA_s = dscr("vA_s", [NKA, 1024])
    gA_s = dscr("gA_s", [NT_OWN, 1024])
    qbT_s = dscr("qbT_s", [8, 128, NT_OWN])
    kbT_s = dscr("kbT_s", [8, 128, NKB])
    vB_s = dscr("vB_s", [NKB, 1024])
    gB_s = dscr("gB_s", [NT_OWN, 1024])
    mixed_s = dscr("mixed_s", [NT_OWN, 2048])
    gate_s = dscr("gate_s", [2048], F32)
    winb_s = nc.dram_tensor("winb_s", [16, 128, 16, 512], BF16, kind="Internal").ap()
    woutb_s = nc.dram_tensor("woutb_s", [4, 128, 16, 512], BF16, kind="Internal").ap()

    es = ExitStack()
    ARENA = 200 * 1024
    art = es.enter_context(nc.sbuf_tensor("arena", [128, ARENA], U8))
    A = Arena(art, ARENA)
    ps_t = es.enter_context(nc.psum_tensor("psall", [128, 8, 512], F32))
    ps_bf_t = ps_t.bitcast(BF16)
    ps = ps_t
    psb = ps_bf_t

    Pg = Prog(nc)
    add = Pg.add
    import os as _os0
    if _os0.environ.get("K_MAXOPS"):
        Pg.limit = int(_os0.environ["K_MAXOPS"])

    ident = A.alloc([128, 128], BF16)
    perm = A.alloc([128, 128], BF16)
    modv = A.alloc([128, 48, 2], F32)
    a_x = A.alloc([128, 16], F32)
    a_c = A.alloc([128, 16], F32)
    nlam = A.alloc([128, 1], F32)
    base_persist = A.off

    add("pool", lambda e: e.dma_start(out=ident, in_=ident_d), writes=["ident"], dma=True)
    add("pool", lambda e: e.dma_start(out=perm, in_=perm_d), writes=["perm"], dma=True)
    def conv_w(cst):
        n = 0
        for (srcw, dstw, nblk, key) in ((win_d, winb_s, 16, "winb"), (wout_d, woutb_s, 4, "woutb")):
            for blk in range(nblk):
                sl = n % 2
                n += 1
                src = srcw[:, blk * 512:(blk + 1) * 512].rearrange("(kc p) n -> p kc n", p=128)
                add("pool", lambda e, sl=sl, src=src: e.dma_start(out=cst[sl], in_=src), writes=[("cst", sl)], dma=True)
                add("sp", lambda e, sl=sl, blk=blk, dstw=dstw: e.dma_start(out=dstw[blk], in_=cst[sl]), reads=[("cst", sl)], writes=[(key, blk)], dma=True)

    if "A" in phases:
        cv = A.alloc([128, 16, 2], F32)
        sc = A.alloc([128, 16, 2], BF16)
        bm = A.alloc([128, 48], F32)
        ng = A.alloc([128, 16], F32)
        lamv = A.alloc([128, 4, 64], F32)
        lprod = A.alloc([128, 2, 64], F32)
        lsum = A.alloc([128, 2], F32)
        wm = [A.alloc([128, 6144], BF16) for _ in range(3)]
        add("sp", lambda e: e.dma_start(out=cv, in_=cvec_d), writes=["cv"], dma=True)
        add("sp", lambda e: e.dma_start(out=bm, in_=bmod_d), writes=["bm"], dma=True)
        add("sp", lambda e: e.dma_start(out=ng, in_=ng_d), writes=["ng"], dma=True)
        add("sp", lambda e: e.dma_start(out=lamv, in_=lamv_d), writes=["lamv"], dma=True)
        add("act", lambda e: e.activation(out=sc, in_=cv, func=AF.Silu), reads=["cv"], writes=["sc"])
        for kc in range(16):
            s = kc % 3
            add("pool", lambda e, s=s, kc=kc: e.dma_start(out=wm[s], in_=wmod_d[kc * 128:(kc + 1) * 128, :]),
                writes=[("wm", s)], dma=True)

            def mm(e, s=s, kc=kc):
                ins = None
                for j in range(48):
                    ins = e.matmul(ps[:, 7, 2 * j:2 * j + 2], lhsT=wm[s][:, j * 128:(j + 1) * 128], rhs=sc[:, kc, :],
                                   start=(kc == 0 and j == 0), stop=(kc == 15), skip_group_check=True)
                return ins
            add("pe", mm, reads=[("wm", s), "sc"], writes=["psM"], banks=[7])
        psM = ps[:, 7, 0:96].rearrange("p (j c) -> p j c", c=2)
        add("dve", lambda e: e.tensor_tensor(out=modv[:, :, 0], in0=psM[:, :, 0], in1=bm, op=ALU.add), reads=["psM", "bm"], writes=["modv0"], banks=[7])
        add("dve", lambda e: e.tensor_tensor(out=modv[:, :, 1], in0=psM[:, :, 1], in1=bm, op=ALU.add), reads=["psM", "bm"], writes=["modv1"], banks=[7])
        add("dve", lambda e: e.scalar_tensor_tensor(out=a_x, in0=modv[:, 16:32, 0], scalar=1.0, in1=ng, op0=ALU.add, op1=ALU.mult),
            reads=["modv0", "ng"], writes=["a_x"])
        add("dve", lambda e: e.scalar_tensor_tensor(out=a_c, in0=modv[:, 16:32, 1], scalar=1.0, in1=ng, op0=ALU.add, op1=ALU.mult),
            reads=["modv1", "ng"], writes=["a_c"])
        import os as _os
        if "gates" not in _os.environ.get("K_SKIP", ""):
            add("sp", lambda e: e.dma_start(out=gate_s.rearrange("(j p) -> p j", p=128), in_=modv[:, 32:48, 0],
                                            allow_slow_non_contiguous=True), reads=["modv0"], writes=["gate_s"], dma=True)
        add("dve", lambda e: e.tensor_tensor(out=lprod[:, 0, :], in0=lamv[:, 0, :], in1=lamv[:, 1, :], op=ALU.mult), reads=["lamv"], writes=["lp0"])
        add("dve", lambda e: e.tensor_tensor(out=lprod[:, 1, :], in0=lamv[:, 2, :], in1=lamv[:, 3, :], op=ALU.mult), reads=["lamv"], writes=["lp1"])
        add("dve", lambda e: e.reduce_sum(out=lsum, in_=lprod, axis=AX.X), reads=["lp0", "lp1"], writes=["lsum"])
        add("act", lambda e: e.activation(out=lsum, in_=lsum, func=AF.Exp), reads=["lsum"], writes=["lsum"])
        add("dve", lambda e: e.scalar_tensor_tensor(out=nlam, in0=lsum[:, 1:2], scalar=-LAMBDA_INIT, in1=lsum[:, 0:1], op0=ALU.add, op1=ALU.subtract),
            reads=["lsum"], writes=["nlam"])
        Pg.barrier()
    b_x = modv[:, 0:16, 0]
    b_c = modv[:, 0:16, 1]

    if "B" in phases:
        A.reset(base_persist)
        xt = [A.alloc([128, 2048], F32) for _ in range(2)]
        xs = [A.alloc([128, 2048], BF16) for _ in range(2)]
        junk = A.alloc([128, 2048], BF16)
        ssq = [A.alloc([128, 1], F32) for _ in range(2)]
        hxT = [A.alloc([128, 16, 512], BF16) for _ in range(2)]
        NW = 4
        wt = [A.alloc([128, 16, 512], BF16) for _ in range(NW)]
        ropeR = A.alloc([128, 2, 128], F32)
        ropeC = A.alloc([128, 2, 64], F32)
        cosT = A.alloc([128, 8, 64], F32)
        sinT = A.alloc([128, 8, 64], F32)
        subg = A.alloc([128, 1024], F32)
        NST = 6
        stg = [A.alloc([128, 512], BF16) for _ in range(NST)]
        q16 = [A.alloc([128, 512], BF16) for _ in range(2)]
        t1 = [A.alloc([128, 512], F32) for _ in range(2)]
        t2 = [A.alloc([128, 512], F32) for _ in range(2)]
        gt = [A.alloc([128, 512], F32) for _ in range(2)]
        add("sp", lambda e: e.dma_start(out=ropeR, in_=ropeR_d), writes=["ropeR"], dma=True)
        add("sp", lambda e: e.dma_start(out=ropeC, in_=ropeC_d), writes=["ropeC"], dma=True)
        add("sp", lambda e: e.dma_start(out=subg, in_=subg_d), writes=["subg"], dma=True)
        add("dve", lambda e: e.tensor_scalar(out=subg, in0=subg, scalar1=1.0 - LAMBDA_INIT, scalar2=None, op0=ALU.mult), reads=["subg"], writes=["subg"])

        st = {"x": 0, "stg": 0, "acc": 0, "r": 0, "w": 0, "k": 0, "ev": 0}
        cosT_f = cosT.rearrange("p a b -> p (a b)")
        sinT_f = sinT.rearrange("p a b -> p (a b)")

        def norm_sub(src, s, hx, hkey, av, bv):
            sl = st["x"] % 2
            st["x"] += 1
            add("sp", lambda e: e.dma_start(out=xt[sl], in_=src[s * 128:(s + 1) * 128, :]), writes=[("xt", sl)], dma=True)
            add("act", lambda e: e.activation(out=junk, in_=xt[sl], func=AF.Square, accum_out=ssq[sl]), reads=[("xt", sl)], writes=[("ssq", sl)])
            add("dve", lambda e: e.tensor_scalar(out=ssq[sl], in0=ssq[sl], scalar1=1.0 / 2048, scalar2=1e-6, op0=ALU.mult, op1=ALU.add),
                reads=[("ssq", sl)], writes=[("ssq", sl)])
            add("act", lambda e: e.activation(out=ssq[sl], in_=ssq[sl], func=AF.Ln), reads=[("ssq", sl)], writes=[("ssq", sl)])
            add("act", lambda e: e.activation(out=ssq[sl], in_=ssq[sl], func=AF.Exp, scale=-0.5), reads=[("ssq", sl)], writes=[("ssq", sl)])
            add("dve", lambda e: e.tensor_scalar(out=xs[sl], in0=xt[sl], scalar1=ssq[sl][:, 0:1], scalar2=None, op0=ALU.mult),
                reads=[("xt", sl), ("ssq", sl)], writes=[("xs", sl)])

            def tr(e):
                ins = None
                for kc in range(16):
                    ins = e.transpose(out=psb[:, kc // 8, (kc % 8) * 128:(kc % 8 + 1) * 128], in_=xs[sl][:, kc * 128:(kc + 1) * 128], identity=ident)
                return ins
            add("pe", tr, reads=[("xs", sl), "ident"], writes=["psX"], banks=[0, 1])
            for kc in range(16):
                src_ps = psb[:, kc // 8, (kc % 8) * 128:(kc % 8 + 1) * 128]
                dst = hxT[hx][:, kc, s * 128:(s + 1) * 128]
                if kc < 8:
                    add("act", lambda e, src_ps=src_ps, dst=dst, kc=kc: e.activation(out=dst, in_=src_ps, func=AF.Identity, scale=av[:, kc:kc + 1], bias=bv[:, kc:kc + 1]),
                        reads=["psX", "a_x", "a_c", "modv0", "modv1"], writes=[(hkey, s, kc)], banks=[kc // 8])
                else:
                    add("dve", lambda e, src_ps=src_ps, dst=dst, kc=kc: e.tensor_scalar(out=dst, in0=src_ps, scalar1=av[:, kc:kc + 1], scalar2=bv[:, kc:kc + 1], op0=ALU.mult, op1=ALU.add),
                        reads=["psX", "a_x", "a_c", "modv0", "modv1"], writes=[(hkey, s, kc)], banks=[kc // 8])

        def hx_reads(hkey, subs):
            return [(hkey, s, kc) for s in subs for kc in range(16)]

        wb_pending = []

        def load_w(c0, first):
            sl = st["w"] % NW
            st["w"] += 1
            blk = c0 // 512
            if first:
                src = win_d[:, c0:c0 + 512].rearrange("(kc p) n -> p kc n", p=128)
                add("pool", lambda e: e.dma_start(out=wt[sl], in_=src), writes=[("wt", sl)], dma=True)
                while wb_pending:
                    wb_pending.pop(0)()
                wb_pending.append(lambda: add("pool", lambda e: e.dma_start(out=winb_s[blk], in_=wt[sl]), reads=[("wt", sl)], writes=[("winb", blk)], dma=True))
            else:
                while wb_pending:
                    wb_pending.pop(0)()
                add("sp", lambda e: e.dma_start(out=wt[sl], in_=winb_s[blk]), reads=[("winb", blk)], writes=[("wt", sl)], dma=True)
            return sl

        def next_acc():
            b = 2 + st["acc"] % 3
            st["acc"] += 1
            return b

        def next_stg():
            k = st["stg"] % NST
            st["stg"] += 1
            return k

        pending = []

        def flush_pending():
            while pending:
                pending.pop(0)()

        def store(k, dst, T=512, rows=128):
            add("pool", lambda e: e.dma_start(out=dst, in_=stg[k][0:rows, 0:T]), reads=[("stg", k)], writes=[("scr", st["k"])], dma=True)
            st["k"] += 1

        def fm_block(wsl, hb, hx, hkey, T, dst, rope):
            while len(pending) > 2:
                pending.pop(0)()
            bank = next_acc()
            nsub = T // 128

            def mm(e):
                ins = None
                for kc in range(16):
                    ins = e.matmul(ps[:, bank, 0:T], lhsT=wt[wsl][:, kc, hb * 128:(hb + 1) * 128], rhs=hxT[hx][:, kc, 0:T], start=(kc == 0), stop=(kc == 15))
                return ins
            add("pe", mm, reads=[("wt", wsl)] + hx_reads(hkey, range(nsub)), writes=[("acc", bank)], banks=[bank])
            if not rope:
                def post():
                    k = next_stg()
                    add("act", lambda e: e.copy(out=stg[k][:, 0:T], in_=ps[:, bank, 0:T]), reads=[("acc", bank)], writes=[("stg", k)], banks=[bank])
                    store(k, dst, T)
                pending.append(post)
            else:
                r = st["r"] % 2
                st["r"] += 1
                rb = 5 + r

                def post():
                    k = next_stg()
                    add("act", lambda e: e.copy(out=q16[r][:, 0:T], in_=ps[:, bank, 0:T]), reads=[("acc", bank)], writes=[("q16", r)], banks=[bank])
                    add("pe", lambda e: e.matmul(ps[:, rb, 0:T], lhsT=perm, rhs=q16[r][:, 0:T], start=True, stop=True),
                        reads=[("q16", r), "perm"], writes=[("psR", rb)], banks=[rb])
                    add("dve", lambda e: e.tensor_tensor(out=t1[r][:, 0:T], in0=ps[:, bank, 0:T], in1=cosT_f[:, 0:T], op=ALU.mult),
                        reads=[("acc", bank), "cosT"], writes=[("t1", r)], banks=[bank])
                    add("dve", lambda e: e.tensor_tensor(out=t2[r][:, 0:T], in0=ps[:, rb, 0:T], in1=sinT_f[:, 0:T], op=ALU.mult),
                        reads=[("psR", rb), "sinT"], writes=[("t2", r)], banks=[rb])
                    add("pool", lambda e: e.tensor_tensor(out=stg[k][:, 0:T], in0=t1[r][:, 0:T], in1=t2[r][:, 0:T], op=ALU.add),
                        reads=[("t1", r), ("t2", r)], writes=[("stg", k)])
                    store(k, dst, T)
                pending.append(post)

        def tm_block(wsl, s, hx, hkey, dst, kind, c0g):
            while len(pending) > 2:
                pending.pop(0)()
            bank = next_acc()

            def mm(e):
                ins = None
                for kc in range(16):
                    ins = e.matmul(ps[:, bank, :], lhsT=hxT[hx][:, kc, s * 128:(s + 1) * 128], rhs=wt[wsl][:, kc, :], start=(kc == 0), stop=(kc == 15))
                return ins
            add("pe", mm, reads=[("wt", wsl)] + hx_reads(hkey, [s]), writes=[("acc", bank)], banks=[bank])

            def post():
                k = next_stg()
                if kind == "v":
                    add("dve", lambda e: e.tensor_copy(out=stg[k], in_=ps[:, bank, :]), reads=[("acc", bank)], writes=[("stg", k)], banks=[bank])
                elif kind == "gb":
                    add("act", lambda e: e.activation(out=stg[k], in_=ps[:, bank, :], func=AF.Silu), reads=[("acc", bank)], writes=[("stg", k)], banks=[bank])
                else:
                    g = st["ev"] % 2
                    st["ev"] += 1
                    add("act", lambda e: e.activation(out=gt[g], in_=ps[:, bank, :], func=AF.Silu), reads=[("acc", bank)], writes=[("gt", g)], banks=[bank])
                    add("pool", lambda e: e.tensor_tensor(out=stg[k], in0=gt[g], in1=subg[:, c0g:c0g + 512], op=ALU.mult),
                        reads=[("gt", g), "subg"], writes=[("stg", k)])
                store(k, dst)
            pending.append(post)

        def rope_tables(t):
            for r in range(8):
                rho = 8 * t + r
                add("dve", lambda e, r=r, rho=rho: e.tensor_scalar(out=cosT[:, r, :], in0=ropeC[:, 0, :], scalar1=ropeR[:, 0, rho:rho + 1], scalar2=None, op0=ALU.add),
                    reads=["ropeR", "ropeC"], writes=["cosT"])
                add("dve", lambda e, r=r, rho=rho: e.tensor_scalar(out=sinT[:, r, :], in0=ropeC[:, 1, :], scalar1=ropeR[:, 1, rho:rho + 1], scalar2=None, op0=ALU.add),
                    reads=["ropeR", "ropeC"], writes=["sinT"])

        tiles = []
        for t in range(16):
            tiles.append(("lat", t))
        tiles.append(("ctx", 0))
        import os as _os
        _kt = _os.environ.get("K_TILES")
        if _kt:
            tiles = [tiles[int(i)] for i in _kt.split(",")]
        _kb = _os.environ.get("K_BLOCKS")

        def tile_src(tl):
            if tl[0] == "lat":
                return x_d[tl[1] * 512:(tl[1] + 1) * 512, :], 4, a_x, b_x
            return ctx_d, 2, a_c, b_c

        def tile_blocks(tl):
            kind, t = tl
            if _kb:
                return [(int(c), None) for c in _kb.split(",")]
            if kind == "lat" and t < 8:
                return [(c, None) for c in range(0, 8192, 512)]
            blocks = [(1024, None), (1536, None), (2048, None), (2560, None)]
            if kind == "ctx" or t == 8:
                blocks += [(5120, None), (5632, None), (6144, None), (6656, None)]
            return blocks

        work = [(ti, c0) for ti, tl in enumerate(tiles) for (c0, _) in tile_blocks(tl)]
        wslots = {}
        wi = {"n": 0}

        def prefetch_w(upto):
            while wi["n"] < min(upto, len(work)):
                wslots[wi["n"]] = load_w(work[wi["n"]][1], work[wi["n"]][0] == 0)
                wi["n"] += 1
        widx = {"n": 0}
        src0, ns0, av0, bv0 = tile_src(tiles[0])
        for s in range(ns0):
            norm_sub(src0, s, 0, ("hx", 0), av0, bv0)
        for ti, tl in enumerate(tiles):
            hx = ti % 2
            hkey = ("hx", hx)
            kind, t = tl
            T = 512 if kind == "lat" else 256
            nsub = T // 128
            tok0 = t * 512 if kind == "lat" else NT_ALL
            if kind == "lat":
                rope_tables(t)
            blocks = tile_blocks(tl)
            nxt = tiles[ti + 1] if ti + 1 < len(tiles) else None
            nxt_subs = []
            if nxt is not None:
                srcn, nsn, avn, bvn = tile_src(nxt)
                nxt_subs = list(range(nsn))
            step = max(1, len(blocks) // (len(nxt_subs) + 1)) if nxt_subs else 0
            for bi, (c0, _) in enumerate(blocks):
                prefetch_w(widx["n"] + 3)
                wsl = wslots[widx["n"]]
                widx["n"] += 1
                grp = c0 // 1024
                hb0 = (c0 % 1024) // 128
                is_own = (kind == "lat" and t < 8)
                if grp in (0, 1, 4, 5):
                    for hb in range(4):
                        h = hb0 + hb
                        if grp == 0:
                            fm_block(wsl, hb, hx, hkey, T, qaT_s[h][:, tok0:tok0 + T], True)
                        elif grp == 1:
                            fm_block(wsl, hb, hx, hkey, T, kaT_s[h][:, tok0:tok0 + T], kind == "lat")
                        elif grp == 4:
                            fm_block(wsl, hb, hx, hkey, T, qbT_s[h][:, tok0:tok0 + T], False)
                        else:
                            if is_own:
                                fm_block(wsl, hb, hx, hkey, T, kbT_s[h][:, tok0:tok0 + T], False)
                            elif kind == "lat":
                                fm_block(wsl, hb, hx, hkey, 256, kbT_s[h][:, NT_OWN:NT_OWN + 256], False)
                            else:
                                fm_block(wsl, hb, hx, hkey, 256, kbT_s[h][:, NT_OWN + 256:NT_OWN + 512], False)
                else:
                    cc = c0 % 1024
                    for s in range(nsub):
                        if grp == 2:
                            tm_block(wsl, s, hx, hkey, vA_s[tok0 + s * 128:tok0 + (s + 1) * 128, cc:cc + 512], "v", cc)
                        elif grp == 3:
                            tm_block(wsl, s, hx, hkey, gA_s[tok0 + s * 128:tok0 + (s + 1) * 128, cc:cc + 512], "ga", cc)
                        elif grp == 7:
                            tm_block(wsl, s, hx, hkey, gB_s[tok0 + s * 128:tok0 + (s + 1) * 128, cc:cc + 512], "gb", cc)
                        else:
                            if is_own:
                                tm_block(wsl, s, hx, hkey, vB_s[tok0 + s * 128:tok0 + (s + 1) * 128, cc:cc + 512], "v", cc)
                            elif kind == "lat":
                                if s < 2:
                                    tm_block(wsl, s, hx, hkey, vB_s[NT_OWN + s * 128:NT_OWN + (s + 1) * 128, cc:cc + 512], "v", cc)
                            else:
                                tm_block(wsl, s, hx, hkey, vB_s[NT_OWN + 256 + s * 128:NT_OWN + 256 + (s + 1) * 128, cc:cc + 512], "v", cc)
                if nxt_subs and (bi + 1) % step == 0:
                    s = nxt_subs.pop(0)
                    norm_sub(srcn, s, 1 - hx, ("hx", 1 - hx), avn, bvn)
            flush_pending()
            while nxt_subs:
                s = nxt_subs.pop(0)
                norm_sub(srcn, s, 1 - hx, ("hx", 1 - hx), avn, bvn)
        Pg.barrier()

    if "C" in phases:
        A.reset(base_persist)
        kaT = [A.alloc([128, NKA], BF16) for _ in range(2)]
        vaA = [A.alloc([128, 66, 129], BF16) for _ in range(2)]
        qt = [A.alloc([128, 512], BF16) for _ in range(2)]
        Gt = [A.alloc([128, 4, 128], BF16) for _ in range(2)]
        NP = 3
        Pt = [A.alloc([128, 2, 512], BF16) for _ in range(NP)]
        rl = A.alloc([128, 4, 2], F32)
        nl2 = A.alloc([128, 4], F32)
        t1c = A.alloc([128, 128], F32)
        Dd = A.alloc([128, 4, 128], F32)
        sqj = A.alloc([128, 128], F32)
        ssq4 = A.alloc([128, 4], F32)
        mst = [A.alloc([128, 4, 128], BF16) for _ in range(2)]
        for i in range(2):
            add("pool", lambda e, i=i: e.memset(vaA[i][:, :, 128:129], 1.0), writes=[("va1", i)])
        psS = [ps[:, 0:2, :], ps[:, 2:4, :]]

        def ogrp(g):
            return ps[:, 4 + g // 3, (g % 3) * 129:(g % 3) * 129 + 129]
        cnt = {"s": 0, "p": 0, "q": 0}
        Osb = A.alloc([128, 1032], F32)
        deferred = []

        def osg(g):
            return Osb[:, g * 129:g * 129 + 129]

        def make_post(h, qb, qs):
            def post():
                ms = (h * 8 + qb) % 2
                okeys = [("Osb", 0), ("Osb", 1), ("Osb", 2)]
                for s in range(4):
                    o1 = osg(s)
                    o2 = osg(4 + s)
                    add("dve", lambda e, s=s, o1=o1: e.reciprocal(out=rl[:, s, 0:1], in_=o1[:, 128:129]), reads=okeys, writes=[("rl", s)])
                    add("dve", lambda e, s=s, o2=o2: e.reciprocal(out=rl[:, s, 1:2], in_=o2[:, 128:129]), reads=okeys, writes=[("rl2", s)])
                    add("dve", lambda e, s=s: e.tensor_scalar(out=nl2[:, s:s + 1], in0=rl[:, s, 1:2], scalar1=nlam[:, 0:1], scalar2=None, op0=ALU.mult),
                        reads=[("rl2", s), "nlam"], writes=[("nl2", s)])
                    add("dve", lambda e, s=s, o1=o1: e.tensor_scalar(out=t1c, in0=o1[:, 0:128], scalar1=rl[:, s, 0:1], scalar2=None, op0=ALU.mult),
                        reads=okeys + [("rl", s)], writes=["t1c"])
                    add("dve", lambda e, s=s, o2=o2: e.scalar_tensor_tensor(out=Dd[:, s, :], in0=o2[:, 0:128], scalar=nl2[:, s:s + 1], in1=t1c, op0=ALU.mult, op1=ALU.add),
                        reads=okeys + [("nl2", s), "t1c"], writes=[("D", s)])
                    add("dve", lambda e, s=s: e.scalar_tensor_tensor(out=sqj, in0=Dd[:, s, :], scalar=1.0, in1=Dd[:, s, :], op0=ALU.mult, op1=ALU.mult, accum_out=ssq4[:, s:s + 1]),
                        reads=[("D", s)], writes=["ssq4"])
                add("dve", lambda e: e.tensor_scalar(out=ssq4, in0=ssq4, scalar1=1.0 / 128, scalar2=1e-5, op0=ALU.mult, op1=ALU.add),
                    reads=["ssq4"], writes=["ssq4"])
                add("act", lambda e: e.activation(out=ssq4, in_=ssq4, func=AF.Ln), reads=["ssq4"], writes=["ssq4"])
                add("act", lambda e: e.activation(out=ssq4, in_=ssq4, func=AF.Exp, scale=-0.5), reads=["ssq4"], writes=["ssq4"])
                for s in range(4):
                    add("dve", lambda e, s=s, ms=ms, qs=qs: e.scalar_tensor_tensor(out=mst[ms][:, s, :], in0=Dd[:, s, :], scalar=ssq4[:, s:s + 1], in1=Gt[qs][:, s, :], op0=ALU.mult, op1=ALU.mult),
                        reads=[("D", s), "ssq4", ("Gt", qs)], writes=[("mst", ms, s)])
                add("pool", lambda e, h=h, qb=qb, ms=ms: e.dma_start(out=mixed_s[qb * 512:(qb + 1) * 512, h * 128:(h + 1) * 128].rearrange("(s p) d -> p s d", p=128), in_=mst[ms]),
                    reads=[("mst", ms, s) for s in range(4)], writes=[("mixA", h, qb)], dma=True)
            return post
        for h in range(8):
            hs = h % 2
            add("sp", lambda e, h=h, hs=hs: e.dma_start(out=kaT[hs], in_=kaT_s[h]), writes=[("kaT", hs)], dma=True)
            add("sp", lambda e, h=h, hs=hs: e.dma_start(out=vaA[hs][:, :, 0:128], in_=vA_s[:, h * 128:(h + 1) * 128].rearrange("(c p) d -> p c d", p=128)),
                writes=[("vaA", hs)], dma=True)
            for qb in range(8):
                qs = cnt["q"] % 2
                cnt["q"] += 1
                add("sp", lambda e, h=h, qb=qb, qs=qs: e.dma_start(out=qt[qs], in_=qaT_s[h][:, qb * 512:(qb + 1) * 512]), writes=[("qt", qs)], dma=True)
                add("sp", lambda e, h=h, qb=qb, qs=qs: e.dma_start(out=Gt[qs], in_=gA_s[qb * 512:(qb + 1) * 512, h * 128:(h + 1) * 128].rearrange("(s p) d -> p s d", p=128)),
                    writes=[("Gt", qs)], dma=True)

                def S_op(kc, hs=hs, qs=qs):
                    sl = cnt["s"] % 2
                    cnt["s"] += 1

                    def f(e):
                        e.matmul(psS[sl][:, 0, :], lhsT=kaT[hs][0:64, kc * 128:(kc + 1) * 128], rhs=qt[qs][0:64, :], start=True, stop=True)
                        return e.matmul(psS[sl][:, 1, :], lhsT=kaT[hs][64:128, kc * 128:(kc + 1) * 128], rhs=qt[qs][64:128, :], start=True, stop=True)
                    add("pe", f, reads=[("kaT", hs), ("qt", qs)], writes=[("S", sl)], banks=[2 * sl, 2 * sl + 1])
                    return sl
                sls = {0: S_op(0), 1: S_op(1)}
                for kc in range(66):
                    if kc == 12:
                        while deferred:
                            deferred.pop(0)()
                    sl = sls[kc]
                    pl = cnt["p"] % NP
                    cnt["p"] += 1
                    add("act", lambda e, sl=sl, pl=pl: e.activation(out=Pt[pl], in_=psS[sl], func=AF.Exp, scale=QSCALE_A),
                        reads=[("S", sl)], writes=[("P", pl)], banks=[2 * sl, 2 * sl + 1])
                    if kc + 2 < 66:
                        sls[kc + 2] = S_op(kc + 2)

                    def pv(e, kc=kc, pl=pl, hs=hs):
                        ins = None
                        for m in range(2):
                            for s in range(4):
                                g = m * 4 + s
                                ins = e.matmul(ogrp(g), lhsT=Pt[pl][:, m, s * 128:(s + 1) * 128], rhs=vaA[hs][:, kc, :],
                                               start=(kc == 0 and g % 3 == 0), stop=(kc == 65), skip_group_check=True)
                        return ins
                    add("pe", pv, reads=[("P", pl), ("vaA", hs), ("va1", hs)], writes=["O"], banks=[4, 5, 6])
                for bk in range(3):
                    ncol = 387 if bk < 2 else 258
                    add("dve", lambda e, bk=bk, ncol=ncol: e.tensor_copy(out=Osb[:, bk * 387:bk * 387 + ncol], in_=ps[:, 4 + bk, 0:ncol]),
                        reads=["O"], writes=[("Osb", bk)], banks=[4 + bk])
                deferred.append(make_post(h, qb, qs))
        while deferred:
            deferred.pop(0)()
        Pg.barrier()

    if "D" in phases:
        A.reset(base_persist)
        kbT = [A.alloc([128, NKB], BF16) for _ in range(2)]
        vbB = [A.alloc([128, 36, 129], BF16) for _ in range(2)]
        nabt = [A.alloc([128, 3, 640], F32) for _ in range(2)]
        namt = A.alloc([128, 3, 640], F32)
        qtb = [A.alloc([128, 512], BF16) for _ in range(2)]
        Gtb = [A.alloc([128, 4, 128], BF16) for _ in range(2)]
        tmpb = [A.alloc([128, 640], F32) for _ in range(2)]
        Ptb = [A.alloc([128, 896], BF16) for _ in range(3)]
        rlb = A.alloc([128, 4], F32)
        t1b = A.alloc([128, 128], F32)
        mstb = [A.alloc([128, 4, 128], BF16) for _ in range(2)]
        for i in range(2):
            add("pool", lambda e, i=i: e.memset(vbB[i][:, :, 128:129], 1.0), writes=[("vb1", i)])
        add("sp", lambda e: e.dma_start(out=namt, in_=nam_d), writes=["namt"], dma=True)
        psSb = [ps[:, 0:2, :].rearrange("p a b -> p (a b)"), ps[:, 2:4, :].rearrange("p a b -> p (a b)")]

        def ogrpb(g):
            return ps[:, 4 + g // 3, (g % 3) * 129:(g % 3) * 129 + 129]
        cn = {"s": 0, "p": 0, "q": 0}
        for h in range(8):
            hs = h % 2
            add("sp", lambda e, h=h, hs=hs: e.dma_start(out=kbT[hs], in_=kbT_s[h]), writes=[("kbT", hs)], dma=True)
            add("sp", lambda e, h=h, hs=hs: e.dma_start(out=vbB[hs][:, :, 0:128], in_=vB_s[:, h * 128:(h + 1) * 128].rearrange("(c p) d -> p c d", p=128)),
                writes=[("vbB", hs)], dma=True)
            add("sp", lambda e, h=h, hs=hs: e.dma_start(out=nabt[hs], in_=nab_d[h]), writes=[("nab", hs)], dma=True)
            add("dve", lambda e, hs=hs: e.tensor_tensor(out=nabt[hs], in0=nabt[hs], in1=namt, op=ALU.add), reads=[("nab", hs), "namt"], writes=[("nab", hs)])
            qslot = {}

            def loads(qg, h=h):
                qs = cn["q"] % 2
                cn["q"] += 1
                qslot[qg] = qs
                add("sp", lambda e: e.dma_start(out=qtb[qs], in_=qbT_s[h][:, qg * 512:(qg + 1) * 512]), writes=[("qtb", qs)], dma=True)
                add("sp", lambda e: e.dma_start(out=Gtb[qs], in_=gB_s[qg * 512:(qg + 1) * 512, h * 128:(h + 1) * 128].rearrange("(s p) d -> p s d", p=128)),
                    writes=[("Gtb", qs)], dma=True)
            info = {}

            def issue_S(i, hs=hs):
                qg, s = i // 4, i % 4
                if qg not in qslot:
                    loads(qg)
                qs = qslot[qg]
                tq = i
                cs = max(tq - 2, 0)
                pat = tq if tq < 2 else 2
                sl = cn["s"] % 2
                cn["s"] += 1
                chunks = [cs + ci for ci in range(5)] + [34, 35]

                def sm(e):
                    ins = None
                    for c, ch in enumerate(chunks):
                        ins = e.matmul(psSb[sl][:, c * 128:(c + 1) * 128], lhsT=kbT[hs][:, ch * 128:(ch + 1) * 128], rhs=qtb[qs][:, s * 128:(s + 1) * 128], start=True, stop=True)
                    return ins
                add("pe", sm, reads=[("kbT", hs), ("qtb", qs)], writes=[("Sb", sl)], banks=[2 * sl, 2 * sl + 1])
                info[i] = (sl, chunks, pat)
            issue_S(0)
            issue_S(1)
            for i in range(32):
                qg, s = i // 4, i % 4
                qs = qslot[qg]
                sl, chunks, pat = info[i]
                ms = (h * 8 + qg) % 2
                tb = cn["p"] % 2
                pl = cn["p"] % 3
                cn["p"] += 1
                add("dve", lambda e, sl=sl, tb=tb, hs=hs, pat=pat: e.scalar_tensor_tensor(out=tmpb[tb], in0=psSb[sl][:, 0:640], scalar=QSCALE_B, in1=nabt[hs][:, pat, :], op0=ALU.mult, op1=ALU.add),
                    reads=[("Sb", sl), ("nab", hs)], writes=[("tmpb", tb)], banks=[2 * sl, 2 * sl + 1])
                add("act", lambda e, sl=sl, pl=pl: e.activation(out=Ptb[pl][:, 640:896], in_=psSb[sl][:, 640:896], func=AF.Exp, scale=QSCALE_B), reads=[("Sb", sl)], writes=[("Pbc", pl)], banks=[2 * sl + 1])
                add("act", lambda e, tb=tb, pl=pl: e.activation(out=Ptb[pl][:, 0:640], in_=tmpb[tb], func=AF.Exp), reads=[("tmpb", tb)], writes=[("Pbw", pl)])
                if i + 2 < 32:
                    issue_S(i + 2)

                def pvb(e, pl=pl, chunks=chunks, hs=hs, s=s):
                    ins = None
                    for c, ch in enumerate(chunks):
                        ins = e.matmul(ogrpb(s), lhsT=Ptb[pl][:, c * 128:(c + 1) * 128], rhs=vbB[hs][:, ch, :], start=(c == 0 and s % 3 == 0), stop=(c == 6), skip_group_check=True)
                    return ins
                add("pe", pvb, reads=[("Pbw", pl), ("Pbc", pl), ("vbB", hs), ("vb1", hs)], writes=[("Ob", s)], banks=[4 + s // 3])
                if s == 3:
                    for s2 in range(4):
                        ob = ogrpb(s2)
                        add("dve", lambda e, s2=s2, ob=ob: e.reciprocal(out=rlb[:, s2:s2 + 1], in_=ob[:, 128:129]), reads=[("Ob", s2)], writes=[("rlb", s2)], banks=[4 + s2 // 3])
                        add("dve", lambda e, s2=s2, ob=ob, ms=ms, qs=qs: e.scalar_tensor_tensor(out=mstb[ms][:, s2, :], in0=ob[:, 0:128], scalar=rlb[:, s2:s2 + 1], in1=Gtb[qs][:, s2, :], op0=ALU.mult, op1=ALU.mult),
                            reads=[("Ob", s2), ("rlb", s2), ("Gtb", qs)], writes=[("mstb", ms, s2)], banks=[4 + s2 // 3])
                    add("pool", lambda e, h=h, qg=qg, ms=ms: e.dma_start(out=mixed_s[qg * 512:(qg + 1) * 512, 1024 + h * 128:1024 + (h + 1) * 128].rearrange("(s p) d -> p s d", p=128), in_=mstb[ms]),
                        reads=[("mstb", ms, s2) for s2 in range(4)], writes=[("mixB", h, qg)], dma=True)
        Pg.barrier()

    if "E" in phases:
        A.reset(base_persist)
        wo = A.alloc([128, 16, 2048], BF16)
        gbc = A.alloc([128, 2048], F32)
        fgs = A.alloc([128, 2048], F32)
        mt = [A.alloc([128, 2048], BF16) for _ in range(2)]
        mT = [A.alloc([128, 16, 128], BF16) for _ in range(2)]
        xe = [A.alloc([128, 2048], F32) for _ in range(2)]
        hsb = [A.alloc([128, 2048], F32) for _ in range(2)]
        ob_ = [A.alloc([128, 2048], F32) for _ in range(2)]
        junk2 = A.alloc([128, 2048], BF16)
        sse = [A.alloc([128, 1], F32) for _ in range(2)]
        for blk in range(4):
            add("pool", lambda e, blk=blk: e.dma_start(out=wo[:, :, blk * 512:(blk + 1) * 512], in_=wout_d[:, blk * 512:(blk + 1) * 512].rearrange("(kc p) n -> p kc n", p=128)), writes=[("wo", blk)], dma=True)
        add("sp", lambda e: e.dma_start(out=gbc, in_=gate_s.partition_broadcast(128)), reads=["gate_s"], writes=["gbc"], dma=True)
        add("sp", lambda e: e.dma_start(out=fgs, in_=fgb_d), writes=["fgs"], dma=True)
        psY = ps[:, 4:8, :].rearrange("p a b -> p (a b)")
        def e_head(ti):
            sl = ti % 2
            add("sp", lambda e, ti=ti, sl=sl: e.dma_start(out=mt[sl], in_=mixed_s[ti * 128:(ti + 1) * 128, :]), writes=[("mt", sl)], dma=True)
            add("sp", lambda e, ti=ti, sl=sl: e.dma_start(out=xe[sl], in_=x_d[ti * 128:(ti + 1) * 128, :]), writes=[("xe", sl)], dma=True)

            def tr(e, sl=sl):
                ins = None
                for kc in range(16):
                    ins = e.transpose(out=psb[:, kc // 8, (kc % 8) * 128:(kc % 8 + 1) * 128], in_=mt[sl][:, kc * 128:(kc + 1) * 128], identity=ident)
                return ins
            add("pe", tr, reads=[("mt", sl), "ident"], writes=["psXe"], banks=[0, 1])
            add("act", lambda e, sl=sl: e.copy(out=mT[sl][:, 0:8, :], in_=psb[:, 0, :].rearrange("p (a b) -> p a b", b=128)), reads=["psXe"], writes=[("mT0", sl)], banks=[0])
            add("dve", lambda e, sl=sl: e.tensor_copy(out=mT[sl][:, 8:16, :], in_=psb[:, 1, :].rearrange("p (a b) -> p a b", b=128)), reads=["psXe"], writes=[("mT1", sl)], banks=[1])


        def e_mid(ti):
            sl = ti % 2
            for hf in range(2):
                def mm(e, sl=sl, hf=hf):
                    ins = None
                    for nb in range(2 * hf, 2 * hf + 2):
                        for kc in range(16):
                            ins = e.matmul(ps[:, 4 + nb, :], lhsT=mT[sl][:, kc, :], rhs=wo[:, kc, nb * 512:(nb + 1) * 512], start=(kc == 0), stop=(kc == 15))
                    return ins
                add("pe", mm, reads=[("mT0", sl), ("mT1", sl)] + [("wo", b) for b in range(4)], writes=[("psY", hf)], banks=[4 + 2 * hf, 5 + 2 * hf])
                psYh = ps[:, 4 + 2 * hf:6 + 2 * hf, :].rearrange("p a b -> p (a b)")
                add("dve", lambda e, sl=sl, hf=hf, psYh=psYh: e.tensor_tensor(out=hsb[sl][:, hf * 1024:(hf + 1) * 1024], in0=psYh, in1=gbc[:, hf * 1024:(hf + 1) * 1024], op=ALU.mult),
                    reads=[("psY", hf), "gbc"], writes=[("hs", sl, hf)], banks=[4 + 2 * hf, 5 + 2 * hf])
                add("pool", lambda e, sl=sl, hf=hf: e.tensor_tensor(out=hsb[sl][:, hf * 1024:(hf + 1) * 1024], in0=hsb[sl][:, hf * 1024:(hf + 1) * 1024], in1=xe[sl][:, hf * 1024:(hf + 1) * 1024], op=ALU.add),
                    reads=[("hs", sl, hf), ("xe", sl)], writes=[("hs", sl, hf)])

        def e_tail(ti):
            sl = ti % 2
            add("act", lambda e, sl=sl: e.activation(out=junk2, in_=hsb[sl], func=AF.Square, accum_out=sse[sl]), reads=[("hs", sl, 0), ("hs", sl, 1)], writes=[("sse", sl)])
            add("dve", lambda e, sl=sl: e.tensor_scalar(out=sse[sl], in0=sse[sl], scalar1=1.0 / 2048, scalar2=1e-6, op0=ALU.mult, op1=ALU.add), reads=[("sse", sl)], writes=[("sse", sl)])
            add("act", lambda e, sl=sl: e.activation(out=sse[sl], in_=sse[sl], func=AF.Ln), reads=[("sse", sl)], writes=[("sse", sl)])
            add("act", lambda e, sl=sl: e.activation(out=sse[sl], in_=sse[sl], func=AF.Exp, scale=-0.5), reads=[("sse", sl)], writes=[("sse", sl)])
            add("dve", lambda e, sl=sl: e.scalar_tensor_tensor(out=ob_[sl], in0=hsb[sl], scalar=sse[sl][:, 0:1], in1=fgs, op0=ALU.mult, op1=ALU.mult),
                reads=[("hs", sl, 0), ("hs", sl, 1), ("sse", sl), "fgs"], writes=[("ob", sl)])
            add("pool", lambda e, ti=ti, sl=sl: e.dma_start(out=out_d[ti * 128:(ti + 1) * 128, :], in_=ob_[sl]), reads=[("ob", sl)], writes=[("out", ti)], dma=True)

        e_head(0)
        for ti in range(32):
            e_mid(ti)
            if ti + 1 < 32:
                e_head(ti + 1)
            e_tail(ti)
        Pg.barrier()

    if Pg.limit is not None:
        print("ops recorded", len(Pg.ops), [(o.eng, o.oid) for o in Pg.ops[-3:]])
    Pg.limit = None
    Pg.barrier()
    add("sp", lambda e: e.wait_ge(Pg.sems[("dma", ("sp", 0))], 16))
    Pg.finalize(es)
    with nc.Block() as block:
        Pg.emit(block)
    es.close()
    return nc


def _grow(half, rho):
    return rho if half == 0 else 127 - rho


def _rope_tables(half):
    f32 = np.float32
    inv = (np.float32(10000.0) ** (-np.arange(16, dtype=f32) / np.float32(16))).astype(f32)
    R = np.zeros((128, 2, 128), f32)
    C = np.zeros((128, 2, 64), f32)
    rows = np.array([_grow(half, r) for r in range(128)], f32)
    cols = np.arange(64, dtype=f32)
    for p in range(128):
        d = p % 64
        fr = inv[p % 16]
        sgn = f32(-1.0) if (p % 32) < 16 else f32(1.0)
        if d < 32:
            ang = (rows * fr).astype(f32)
            R[p, 0] = np.cos(ang).astype(f32)
            R[p, 1] = sgn * np.sin(ang).astype(f32)
        else:
            ang = (cols * fr).astype(f32)
            C[p, 0] = np.cos(ang).astype(f32)
            C[p, 1] = sgn * np.sin(ang).astype(f32)
    return R, C


def _perm_matrix():
    Pm = np.zeros((128, 128), np.float32)
    for p in range(128):
        partner = p + 16 if (p % 32) < 16 else p - 16
        Pm[partner, p] = 1.0
    return Pm


def _na_tables(half, rpb):
    nab = np.zeros((8, 128, 3, 640), np.float32)
    nam = np.full((128, 3, 640), NEG, np.float32)
    j = np.arange(64)
    c0 = np.clip(j - 8, 0, 48)
    kc = np.arange(64)
    incol = (kc[:, None] >= c0[None, :]) & (kc[:, None] <= c0[None, :] + 15)
    coff = kc[:, None] - j[None, :] + 15
    coff_c = np.clip(coff, 0, 30)
    for pi, tq in enumerate((0, 1, 2)):
        cs = max(tq - 2, 0)
        for ci in range(5):
            lc = cs + ci
            for a in range(2):
                kr = _grow(half, 2 * lc + a)
                for dr in range(2):
                    r = _grow(half, 2 * tq + dr)
                    r0 = min(max(r - 4, 0), 120)
                    if not (r0 <= kr <= r0 + 7):
                        continue
                    roff = kr - r + 7
                    sl = slice(ci * 128 + dr * 64, ci * 128 + dr * 64 + 64)
                    nam[a * 64:(a + 1) * 64, pi, sl] = np.where(incol, 0.0, NEG)
                    vals = rpb[:, roff, :][:, coff_c]
                    nab[:, a * 64:(a + 1) * 64, pi, sl] = np.where(incol[None], vals, 0.0)
    return nab, nam


def _core_inputs(b, half, inp, shared):
    f32 = np.float32
    x = inp["x"]
    rows = [_grow(half, r) for r in range(128)]
    x_loc = np.ascontiguousarray(x[b].reshape(128, 64, 2048)[rows].reshape(8192, 2048))
    cvec = np.stack([inp["c"][b].reshape(16, 128).T, inp["c_ctx"].reshape(16, 128).T], axis=-1)
    R, C = shared["rope"][half]
    nab, nam = shared["na"][half]
    return {
        "x_loc": x_loc,
        "ctx": np.ascontiguousarray(inp["ctx"][b]),
        "cvec": np.ascontiguousarray(cvec.astype(f32)),
        "w_mod": shared["w_mod"], "b_mod2": shared["b_mod2"], "norm_g2": shared["norm_g2"],
        "w_in": shared["w_in"], "w_out": shared["w_out"], "lamv": shared["lamv"],
        "sublng": shared["sublng"], "fgb": shared["fgb"],
        "ropeR": R, "ropeC": C, "permm": shared["perm"], "ident": shared["ident"],
        "nab": nab, "nam": nam,
    }


def _shared(inp):
    f32 = np.float32
    sh = {}
    sh["w_mod"] = np.ascontiguousarray(inp["w_mod"][0], dtype=f32)
    sh["b_mod2"] = np.ascontiguousarray(inp["b_mod"][0].reshape(48, 128).T, dtype=f32)
    sh["norm_g2"] = np.ascontiguousarray(inp["norm_g"][0].reshape(16, 128).T, dtype=f32)
    sh["w_in"] = np.ascontiguousarray(inp["w_in"][0], dtype=f32)
    sh["w_out"] = np.ascontiguousarray(inp["w_out"][0], dtype=f32)
    lam = np.stack([inp["lam_q1"][0], inp["lam_k1"][0], inp["lam_q2"][0], inp["lam_k2"][0]], 0)
    sh["lamv"] = np.ascontiguousarray(np.broadcast_to(lam[None], (128, 4, 64)), dtype=f32)
    sh["sublng"] = np.ascontiguousarray(np.broadcast_to(np.tile(inp["subln_g"][0], 8)[None], (128, 1024)), dtype=f32)
    sh["fgb"] = np.ascontiguousarray(np.broadcast_to(inp["final_g"][None], (128, 2048)), dtype=f32)
    sh["perm"] = _perm_matrix()
    sh["ident"] = np.eye(128, dtype=f32)
    sh["rope"] = [_rope_tables(h) for h in range(2)]
    rpb = np.asarray(inp["rpb"][0], dtype=f32)
    sh["na"] = [_na_tables(h, rpb) for h in range(2)]
    return sh


_NC_CACHE = {}


def kernel(**inputs):
    inp = {k: np.asarray(v) for k, v in inputs.items()}
    sh = _shared(inp)
    in_maps = []
    for core in range(8):
        b, half = core // 2, core % 2
        in_maps.append(_core_inputs(b, half, inp, sh))
    if "nc" not in _NC_CACHE:
        _NC_CACHE["nc"] = build_nc()
    nc = _NC_CACHE["nc"]
    res = run_bass_kernel_spmd(nc, in_maps, core_ids=list(range(8)))
    out = np.empty((4, 8192, 2048), np.float32)
    for core in range(8):
        b, half = core // 2, core % 2
        o = np.asarray(res.results[core]["out_loc"]).reshape(64, 64, 2048)
        rows = [_grow(half, r) for r in range(64)]
        out[b].reshape(128, 64, 2048)[rows] = o
    return out
```
